# Optimizing a Trainium2 kernel written in Bass

```python
import jax
import jax.numpy as jnp
from jax import lax


D_MODEL = 1024
BATCH = 16
SEQ = 2048
DEPTH = 1

PLE_DIM = 256
LN_EPS = 1e-5
DEEPNORM_ALPHA = (2 * DEPTH) ** 0.25
DEEPNORM_BETA = (8 * DEPTH) ** -0.25

GM_CHUNK = 128
GM_GROUPS = 8
GM_WIDTH = D_MODEL
GM_GROUP_DIM = GM_WIDTH // GM_GROUPS

SSD_INNER = 2 * D_MODEL
SSD_HEAD_DIM = 64
SSD_HEADS = SSD_INNER // SSD_HEAD_DIM
SSD_GROUPS = 8
SSD_HEADS_PER_GROUP = SSD_HEADS // SSD_GROUPS
SSD_STATE = 128
SSD_CONV = 4
SSD_CHUNK = 128
SSD_CONV_DIM = SSD_INNER + 2 * SSD_GROUPS * SSD_STATE

PEER_HEADS = 8
PEER_NKEYS = 128
PEER_EXPERTS = PEER_NKEYS * PEER_NKEYS
PEER_DKEY = 256
PEER_HALF = PEER_DKEY // 2
PEER_TOPK = 16
PEER_TOKEN_BLOCK = 128

N_BRANCHES = 2
IN_COLS = 2 * GM_WIDTH + SSD_INNER + SSD_CONV_DIM + SSD_HEADS + N_BRANCHES * D_MODEL

kernel_name = 'hybrid_gmlp_ssd_peer_block'


def layer_norm(x, g, b):
    xf = x.astype(jnp.float32)
    mu = jnp.mean(xf, axis=-1, keepdims=True)
    var = jnp.mean(jnp.square(xf - mu), axis=-1, keepdims=True)
    return ((xf - mu) * lax.rsqrt(var + LN_EPS) * g + b).astype(x.dtype)


def gmlp_branch(uv, ln_g, ln_b, w_s, b_s):
    u, v = jnp.split(jax.nn.gelu(uv), 2, axis=-1)
    v = layer_norm(v, ln_g, ln_b)
    bsz, s, _ = v.shape
    nc = s // GM_CHUNK
    v = v.reshape(bsz, nc, GM_CHUNK, GM_GROUPS, GM_GROUP_DIM)
    causal = jnp.tril(jnp.ones((GM_CHUNK, GM_CHUNK), dtype=bool))
    w = jnp.where(causal[None], w_s, jnp.zeros_like(w_s))
    mixed = jnp.einsum('gts,bcsgd->bctgd', w, v) + b_s.T[None, None, :, :, None]
    return u * mixed.reshape(bsz, s, GM_WIDTH).astype(u.dtype)


def causal_depthwise_conv(x, w, b):
    c = x.shape[-1]
    out = lax.conv_general_dilated(
        x, w[:, None, :].astype(x.dtype), window_strides=(1,), padding=[(SSD_CONV - 1, 0)],
        dimension_numbers=('NWC', 'WIO', 'NWC'), feature_group_count=c)
    return out + b


def ssd_branch(z, xbc, dt_raw, conv_w, conv_b, dt_bias, a_log, d_skip, norm_w):
    xbc = jax.nn.silu(causal_depthwise_conv(xbc, conv_w, conv_b))
    xs, bm, cm = jnp.split(xbc, [SSD_INNER, SSD_INNER + SSD_GROUPS * SSD_STATE], axis=-1)
    bsz, s, _ = xs.shape
    nc = s // SSD_CHUNK
    L = SSD_CHUNK
    G, HG, P, N = SSD_GROUPS, SSD_HEADS_PER_GROUP, SSD_HEAD_DIM, SSD_STATE
    x = xs.reshape(bsz, nc, L, G, HG, P)
    bm = bm.reshape(bsz, nc, L, G, N)
    cm = cm.reshape(bsz, nc, L, G, N)
    dt = jax.nn.softplus(dt_raw.astype(jnp.float32) + dt_bias).reshape(bsz, nc, L, G, HG)
    a = -jnp.exp(a_log.astype(jnp.float32)).reshape(G, HG)
    a_cum = jnp.cumsum(dt * a, axis=2)
    a_cum_t = jnp.transpose(a_cum, (0, 1, 3, 4, 2))
    xdt = x * dt[..., None]
    causal = jnp.tril(jnp.ones((L, L), dtype=bool))
    seg = a_cum_t[..., :, None] - a_cum_t[..., None, :]
    decay = jnp.exp(jnp.where(causal, seg, -jnp.inf))
    cb = jnp.einsum('bclgn,bcsgn->bcgls', cm, bm)
    y_diag = jnp.einsum('bcgls,bcghls,bcsghp->bclghp', cb, decay, xdt)
    decay_states = jnp.exp(a_cum_t[..., -1:] - a_cum_t)
    states = jnp.einsum('bclgn,bcghl,bclghp->bcghpn', bm, decay_states, xdt)
    chunk_decay = jnp.exp(a_cum_t[..., -1])

    def step(h, inp):
        dec, st = inp
        return dec[..., None, None] * h + st, h

    h0 = jnp.zeros((bsz, G, HG, P, N), states.dtype)
    _, prev = lax.scan(step, h0, (jnp.moveaxis(chunk_decay, 1, 0), jnp.moveaxis(states, 1, 0)))
    prev = jnp.moveaxis(prev, 0, 1)
    y_off = jnp.einsum('bclgn,bcghpn,bcghl->bclghp', cm, prev, jnp.exp(a_cum_t))
    y = y_diag + y_off + x * d_skip.reshape(G, HG)[..., None]
    y = y.reshape(bsz, s, SSD_INNER)
    yf = (y * jax.nn.silu(z)).astype(jnp.float32)
    yf = yf * lax.rsqrt(jnp.mean(jnp.square(yf), axis=-1, keepdims=True) + LN_EPS) * norm_w
    return yf.astype(z.dtype)


def peer(x, w_q, sub_k1, sub_k2, u_tab, v_tab):
    bsz, s, d = x.shape
    T = bsz * s
    xt = x.reshape(T, d)
    q = (xt @ w_q).reshape(T, PEER_HEADS, 2, PEER_HALF)
    s1 = jnp.einsum('thd,hkd->thk', q[:, :, 0], sub_k1).astype(jnp.float32)
    s2 = jnp.einsum('thd,hkd->thk', q[:, :, 1], sub_k2).astype(jnp.float32)
    v1, i1 = lax.top_k(s1, PEER_TOPK)
    v2, i2 = lax.top_k(s2, PEER_TOPK)
    cand = (v1[..., :, None] + v2[..., None, :]).reshape(T, PEER_HEADS, PEER_TOPK * PEER_TOPK)
    cand_idx = (i1[..., :, None] * PEER_NKEYS + i2[..., None, :]).reshape(T, PEER_HEADS, PEER_TOPK * PEER_TOPK)
    best, pos = lax.top_k(cand, PEER_TOPK)
    experts = jnp.take_along_axis(cand_idx, pos, axis=-1)
    gates = jax.nn.softmax(best, axis=-1)
    nb = T // PEER_TOKEN_BLOCK
    hk = PEER_HEADS * PEER_TOPK

    def block(args):
        xb, eb, gb = args
        u = u_tab[eb]
        act = jax.nn.gelu(jnp.einsum('tkd,td->tk', u, xb).astype(jnp.float32))
        return jnp.einsum('tk,tkd->td', (gb * act).astype(x.dtype), v_tab[eb])

    out = lax.map(block, (xt.reshape(nb, PEER_TOKEN_BLOCK, d),
                          experts.reshape(nb, PEER_TOKEN_BLOCK, hk),
                          gates.reshape(nb, PEER_TOKEN_BLOCK, hk)))
    return out.reshape(bsz, s, d).astype(x.dtype)


def setup_inputs(seed: int = 0) -> dict:
    key = jax.random.key(seed)
    ks = jax.random.split(key, 32)
    L = DEPTH
    f32 = jnp.float32

    def nrm(k, shape, scale):
        return jax.random.normal(k, shape, f32) * scale

    dt0 = jnp.exp(jax.random.uniform(ks[10], (L, SSD_HEADS), f32, jnp.log(1e-3), jnp.log(1e-1)))
    return {
        'x': nrm(ks[0], (BATCH, SEQ, D_MODEL), 1.0),
        'p': nrm(ks[1], (DEPTH, BATCH, SEQ, PLE_DIM), 1.0),
        'w_in': nrm(ks[2], (L, D_MODEL, IN_COLS), D_MODEL ** -0.5),
        'b_gate': nrm(ks[3], (L, N_BRANCHES * D_MODEL), 0.1),
        'gm_ln_g': 1.0 + nrm(ks[4], (L, GM_WIDTH), 0.02),
        'gm_ln_b': nrm(ks[5], (L, GM_WIDTH), 0.02),
        'gm_w_s': nrm(ks[6], (L, GM_GROUPS, GM_CHUNK, GM_CHUNK), GM_CHUNK ** -0.5),
        'gm_b_s': 1.0 + nrm(ks[7], (L, GM_GROUPS, GM_CHUNK), 0.02),
        'gm_w_out': nrm(ks[8], (L, GM_WIDTH, D_MODEL), GM_WIDTH ** -0.5),
        'ssd_conv_w': nrm(ks[9], (L, SSD_CONV, SSD_CONV_DIM), SSD_CONV ** -0.5),
        'ssd_conv_b': nrm(ks[11], (L, SSD_CONV_DIM), 0.02),
        'ssd_dt_bias': dt0 + jnp.log(-jnp.expm1(-dt0)),
        'ssd_a_log': jnp.log(jax.random.uniform(ks[12], (L, SSD_HEADS), f32, 1.0, 16.0)),
        'ssd_d': 1.0 + nrm(ks[13], (L, SSD_HEADS), 0.02),
        'ssd_norm_w': 1.0 + nrm(ks[14], (L, SSD_INNER), 0.02),
        'ssd_w_out': nrm(ks[15], (L, SSD_INNER, D_MODEL), SSD_INNER ** -0.5),
        'w_o': nrm(ks[16], (L, D_MODEL, D_MODEL), D_MODEL ** -0.5 * DEEPNORM_BETA),
        'ln1_g': 1.0 + nrm(ks[17], (L, D_MODEL), 0.02),
        'ln1_b': nrm(ks[18], (L, D_MODEL), 0.02),
        'peer_w_q': nrm(ks[19], (L, D_MODEL, PEER_HEADS * PEER_DKEY), D_MODEL ** -0.5),
        'peer_k1': nrm(ks[20], (L, PEER_HEADS, PEER_NKEYS, PEER_HALF), PEER_HALF ** -0.5),
        'peer_k2': nrm(ks[21], (L, PEER_HEADS, PEER_NKEYS, PEER_HALF), PEER_HALF ** -0.5),
        'peer_u': nrm(ks[22], (L, PEER_EXPERTS, D_MODEL), D_MODEL ** -0.5),
        'peer_v': nrm(ks[23], (L, PEER_EXPERTS, D_MODEL), DEEPNORM_BETA * PEER_HEADS ** -0.5),
        'ple_w_proj': nrm(ks[24], (L, PLE_DIM, D_MODEL), PLE_DIM ** -0.5 * DEEPNORM_BETA),
        'ple_w_gate': nrm(ks[25], (L, D_MODEL, D_MODEL), D_MODEL ** -0.5),
        'ple_b_gate': nrm(ks[26], (L, D_MODEL), 0.1),
        'ln2_g': 1.0 + nrm(ks[27], (L, D_MODEL), 0.02),
        'ln2_b': nrm(ks[28], (L, D_MODEL), 0.02),
    }


def reference(x, p, w_in, b_gate, gm_ln_g, gm_ln_b, gm_w_s, gm_b_s, gm_w_out,
              ssd_conv_w, ssd_conv_b, ssd_dt_bias, ssd_a_log, ssd_d, ssd_norm_w, ssd_w_out,
              w_o, ln1_g, ln1_b, peer_w_q, peer_k1, peer_k2, peer_u, peer_v,
              ple_w_proj, ple_w_gate, ple_b_gate, ln2_g, ln2_b):
    bsz, s, d = x.shape
    o1 = 2 * GM_WIDTH
    o2 = o1 + SSD_INNER
    o3 = o2 + SSD_CONV_DIM
    o4 = o3 + SSD_HEADS
    for i in range(DEPTH):
        h = x @ w_in[i]
        uv, z, xbc, dt_raw, gate_pre = jnp.split(h, [o1, o2, o3, o4], axis=-1)
        gates = jax.nn.sigmoid((gate_pre + b_gate[i]).astype(jnp.float32)).reshape(bsz, s, N_BRANCHES, d)
        ya = gmlp_branch(uv, gm_ln_g[i], gm_ln_b[i], gm_w_s[i], gm_b_s[i]) @ gm_w_out[i]
        yb = ssd_branch(z, xbc, dt_raw, ssd_conv_w[i], ssd_conv_b[i], ssd_dt_bias[i],
                        ssd_a_log[i], ssd_d[i], ssd_norm_w[i]) @ ssd_w_out[i]
        merged = (gates[..., 0, :] * ya + gates[..., 1, :] * yb).astype(x.dtype)
        x1 = layer_norm(DEEPNORM_ALPHA * x + merged @ w_o[i], ln1_g[i], ln1_b[i])
        ch = peer(x1, peer_w_q[i], peer_k1[i], peer_k2[i], peer_u[i], peer_v[i])
        ple_gate = jax.nn.sigmoid((x1 @ ple_w_gate[i] + ple_b_gate[i]).astype(jnp.float32))
        ple = (ple_gate * (p[i] @ ple_w_proj[i])).astype(x.dtype)
        x = layer_norm(DEEPNORM_ALPHA * x1 + ch + ple, ln2_g[i], ln2_b[i])
    return x
```

```python
import numpy as np
import concourse.bass as bass
import concourse.mybir as mybir
from concourse.bass_utils import run_bass_kernel_spmd

dt = mybir.dt
F32, BF16, U32, I32 = dt.float32, dt.bfloat16, dt.uint32, dt.int32
AF = mybir.ActivationFunctionType
ALU = mybir.AluOpType
AX = mybir.AxisListType

EPOCH = 6000
NDS = {"sp": 8, "act": 4, "pool": 24}


class Op:
    __slots__ = ("eng", "fn", "reads", "writes", "dma", "deps", "sig", "idx", "waits", "depops", "psr")

    def __init__(self, eng, fn, reads, writes, dma):
        self.eng, self.fn, self.reads, self.writes, self.dma = eng, fn, reads, writes, dma
        self.deps = set()
        self.sig = None
        self.waits = []
        self.depops = None
        self.psr = ()
        self.idx = -1


class Prog:
    ENG = ("pe", "act", "dve", "pool", "sp")

    def __init__(self, nc):
        self.nc = nc
        self.ops = []
        self.cur = self.ops
        self._stack = []

    def op(self, eng, fn, reads=(), writes=(), dma=False):
        reads, writes = list(reads), list(writes)
        psr = []
        for r in list(reads):
            if isinstance(r, str) and r.startswith("ps") and r not in writes:
                writes.append(r)
                psr.append(r)
        o = Op(eng, fn, tuple(reads), tuple(writes), dma)
        o.psr = tuple(psr)
        self.cur.append(o)
        return o

    def begin(self):
        self._stack.append(self.cur)
        self.cur = []

    def end(self):
        l = self.cur
        self.cur = self._stack.pop()
        return l

    def extend(self, lst):
        self.cur.extend(lst)

    def analyze(self):
        last_w, readers = {}, {}
        for i, o in enumerate(self.ops):
            o.idx = i
        for o in self.ops:
            if o.depops is not None:
                o.deps = {d.idx for d in o.depops}
                continue
            for r in o.reads:
                if r in last_w:
                    o.deps.add(last_w[r])
            for w in o.writes:
                if w in last_w:
                    o.deps.add(last_w[w])
                if w in o.psr:
                    for rd in readers.get(w, ()):
                        if self.ops[rd].eng != o.eng:
                            o.deps.add(rd)
                else:
                    for rd in readers.get(w, ()):
                        o.deps.add(rd)
            for w in o.writes:
                if w in o.psr:
                    readers.setdefault(w, []).append(o.idx)
                else:
                    last_w[w] = o.idx
                    readers[w] = []
            for r in o.reads:
                if r not in o.writes:
                    readers.setdefault(r, []).append(o.idx)
            o.deps.discard(o.idx)

    def dma(self, q, out, in_, reads=(), writes=(), **kw):
        return self.op(q, lambda e: e.dma_start(out=out, in_=in_, **kw), reads, writes, dma=True)

    def finalize(self):
        self.analyze()
        ops = self.ops
        for o in ops:
            if o.eng == "pe" and not o.dma:
                o.deps = {d for d in o.deps if not (ops[d].eng == "pe" and not ops[d].dma)}
        needed = set()
        for o in ops:
            needed |= o.deps
        cnt = {e: 0 for e in self.ENG}
        dcnt = {e: 0 for e in self.ENG}
        self.sem_keys = set()
        for o in ops:
            if o.dma:
                n = dcnt[o.eng]
                dcnt[o.eng] += 1
                nds = NDS[o.eng]
                key = ("d", o.eng, n % nds)
                o.sig = (key, 16 * (n // nds + 1), 16)
                self.sem_keys.add(key)
                if n >= nds:
                    o.waits.append((key, 16 * (n // nds)))
            elif o.idx in needed:
                cnt[o.eng] += 1
                c = cnt[o.eng]
                key = ("c", o.eng, (c - 1) // EPOCH)
                o.sig = (key, (c - 1) % EPOCH + 1, 1)
                self.sem_keys.add(key)
        for o in ops:
            best = {}
            for d in o.deps:
                key, val, _ = ops[d].sig
                if best.get(key, 0) < val:
                    best[key] = val
            for key, val in o.waits:
                if best.get(key, 0) < val:
                    best[key] = val
            o.waits = sorted(best.items(), key=lambda kv: str(kv[0]))

    def emit(self, final_wait_ops=()):
        from contextlib import ExitStack
        self.finalize()
        nc = self.nc
        ops = self.ops
        with ExitStack() as st:
            sems = {}
            for key in sorted(self.sem_keys, key=str):
                sems[key] = st.enter_context(nc.semaphore("s_%s_%s_%d" % key))
            block = st.enter_context(nc.Block())

            def run(engname):
                def body(e):
                    for o in ops:
                        if o.eng != engname:
                            continue
                        for key, val in o.waits:
                            e.wait_ge(sems[key], val)
                        if o.fn is None:
                            continue
                        inst = o.fn(e)
                        if o.sig is not None:
                            inst.then_inc(sems[o.sig[0]], o.sig[2])
                return body

            block.sync(run("sp"))
            block.scalar(run("act"))
            block.vector(run("dve"))
            block.gpsimd(run("pool"))
            block.tensor(run("pe"))

    def barrier_wait(self, eng, dep_ops):
        o = Op(eng, None, (), (), False)
        o.depops = list(dep_ops)
        self.cur.append(o)
        return o


LN_EPS = 1e-5
ALPHA = 2.0 ** 0.25
NEG = -1.0e5


def cp_layout():
    names = [("ident", 128), ("tri", 128), ("negm", 128), ("ones", 128), ("iota16", 16),
             ("bgate", 16), ("gmg", 1024), ("gmb", 1024), ("bs", 1024), ("convw", 128), ("convb", 32),
             ("dtb", 32), ("alog", 32), ("dsk", 16), ("nw", 16),
             ("ln1g", 1024), ("ln1b", 1024), ("ln2g", 1024), ("ln2b", 1024), ("pleb", 1024)]
    off = {}
    o = 0
    for n, w in names:
        off[n] = (o, w)
        o += w
    return off, o


def pack_cp(I):
    off, ncol = cp_layout()
    cp = np.zeros((128, ncol), np.float32)

    def put(n, a):
        o, w = off[n]
        cp[:, o:o + w] = a

    idx = np.arange(128)
    put("ident", np.eye(128, dtype=np.float32))
    put("tri", (idx[:, None] <= idx[None, :]).astype(np.float32))
    put("negm", np.where(idx[:, None] <= idx[None, :], 0.0, NEG).astype(np.float32))
    put("ones", np.ones((128, 128), np.float32))
    put("iota16", np.broadcast_to(np.arange(16, dtype=np.float32), (128, 16)))
    put("bgate", I["b_gate"][0].reshape(16, 128).T)
    put("gmg", np.broadcast_to(I["gm_ln_g"][0], (128, 1024)))
    put("gmb", np.broadcast_to(I["gm_ln_b"][0], (128, 1024)))
    put("bs", np.broadcast_to(I["gm_b_s"][0].reshape(1024), (128, 1024)))
    put("convw", I["ssd_conv_w"][0].reshape(4, 32, 128).transpose(2, 1, 0).reshape(128, 128))
    put("convb", I["ssd_conv_b"][0].reshape(32, 128).T)
    put("dtb", np.broadcast_to(I["ssd_dt_bias"][0], (128, 32)))
    put("alog", np.broadcast_to(I["ssd_a_log"][0], (128, 32)))
    put("dsk", np.repeat(I["ssd_d"][0], 64).reshape(16, 128).T)
    put("nw", I["ssd_norm_w"][0].reshape(16, 128).T)
    put("ln1g", np.broadcast_to(I["ln1_g"][0], (128, 1024)))
    put("ln1b", np.broadcast_to(I["ln1_b"][0], (128, 1024)))
    put("ln2g", np.broadcast_to(I["ln2_g"][0], (128, 1024)))
    put("ln2b", np.broadcast_to(I["ln2_b"][0], (128, 1024)))
    put("pleb", np.broadcast_to(I["ple_b_gate"][0], (128, 1024)))
    return cp


class StopBuild(Exception):
    pass


def build(nseq, nch, taps=(), stop=None, NX1=2, NGB=7):
    from contextlib import ExitStack
    nchunks = nseq * nch
    ntok = nchunks * 128
    nc = bass.Bass("TRN2", target_bir_lowering=False)
    off, NCP = cp_layout()

    def din(name, shape, d=F32):
        return nc.dram_tensor(name, shape, d, kind="ExternalInput").ap()

    x = din("x", [ntok, 1024]); p = din("p", [ntok, 256])
    w_in = din("w_in", [1024, 10272]); w_gm = din("gm_w_out", [1024, 1024]); w_sso = din("ssd_w_out", [2048, 1024])
    w_o = din("w_o", [1024, 1024]); w_q = din("w_q", [1024, 2048]); w_pg = din("w_pg", [1024, 1024]); w_pp = din("w_pp", [256, 1024])
    peer_u = din("peer_u", [16384, 1024]); peer_v = din("peer_v", [16384, 1024])
    cp = din("cp", [128, NCP]); wsT_in = din("wsT", [128, 1024]); kT_in = din("kT", [128, 2048])
    out = nc.dram_tensor("out", [ntok, 1024], F32, kind="ExternalOutput").ap()
    tapo = {}
    for name, shape in taps:
        tapo[name] = nc.dram_tensor("tap_" + name, list(shape), F32, kind="ExternalOutput").ap()

    with ExitStack() as st:
        def sb(name, shape, d=F32):
            return st.enter_context(nc.sbuf_tensor("sb_" + name, shape, d))

        P = Prog(nc)
        ps = [st.enter_context(nc.psum_tensor("ps%d" % i, [128, 512], F32)) for i in range(8)]
        bank_ctr = [0]

        def nb():
            b = bank_ctr[0] % 5
            bank_ctr[0] += 1
            return b

        def blk(j, w=128):
            return slice(j * w, (j + 1) * w)

        cps = sb("cps", [128, NCP])

        def C(n, a=None, b=None):
            o, w = off[n]
            if a is None:
                return cps[:, o:o + w]
            return cps[:, o + a:o + b]

        wsT = sb("wsT", [128, 1024], BF16)
        kT = sb("kT", [128, 2048], BF16)
        a_bc = sb("a_bc", [128, 32])
        NW = 2
        wr = [sb("wr%d" % i, [128, 4096], BF16) for i in range(NW)]
        xT = sb("xT", [128, 1024], BF16); pT = sb("pT", [128, 256], BF16)
        uT = sb("uT", [128, 1024], BF16)
        vv = sb("vv", [128, 1024])
        vnb = sb("vnb", [128, 1024], BF16)
        gT = sb("gT", [128, 2048], BF16); szT = sb("szT", [128, 2048], BF16)
        rawt = [sb("rawt%d" % i, [128, 4 * 131]) for i in range(2)]
        cacc = sb("cacc", [128, 512])
        cconv = [sb("cconv%d" % i, [128, 512]) for i in range(2)]
        xq = [sb("xq%d" % i, [128, 1024]) for i in range(3)]
        BCb = sb("BCb", [128, 2048], BF16)
        halo = sb("halo", [128, 96])
        gaT = sb("gaT", [128, 1024], BF16)
        mA = sb("mA", [128, 1024]); cbT = sb("cbT", [128, 1024], BF16)
        sm = sb("sm", [128, 512])
        sd = sb("sd", [128, 2048])
        MT = sb("MT", [128, 512], BF16); CsT = sb("CsT", [128, 512], BF16)
        xdt = sb("xdt", [128, 2048], BF16)
        B_tm = sb("B_tm", [128, 1024], BF16)
        ytmp = [sb("ytmp%d" % i, [128, 128]) for i in range(2)]
        ysq = [sb("ysq%d" % i, [128, 128]) for i in range(2)]
        ygT = sb("ygT", [128, 2048])
        rstd = sb("rstd", [128, 128])
        ynT = sb("ynT", [128, 2048], BF16); mT = sb("mT", [128, 1024], BF16)
        xdtds = ynT
        H = sb("H", [128, 2048]); HTb = sb("HTb", [128, 2048], BF16)
        x1s = [sb("x1_tm%d" % i, [128, 1024]) for i in range(NX1)]; x1T = sb("x1T", [128, 1024], BF16)
        gbt = [sb("gb%d" % i, [128, 1024], BF16) for i in range(NGB)]
        ple_s = sb("ple_s", [128, 1024])
        eid_p = [sb("eid_p%d" % i, [128, 128], U32) for i in range(2)]
        gates_p = [sb("gates_p%d" % i, [128, 128]) for i in range(2)]
        lnst2 = sb("lnst2", [128, 32])
        tsum_t = sb("tsum", [128, 1024])
        dgt = [sb("dg%d" % i, [128, 128], BF16) for i in range(4)]
        identb = sb("identb", [128, 128], BF16)
        junk = sb("junk", [128, 1024], BF16)
        tk = sb("tk", [128, 2048])
        tki = sb("tki", [128, 768], U32)
        lnst = sb("lnst", [128, 32])

        v_tm = vv[:, 0:1024]; vh = v_tm
        drep, Ebc, segm, tmpL = sd[:, 0:512], sd[:, 512:1024], sd[:, 1024:1536], sd[:, 1536:2048]
        qT = xdt
        x_tm = ygT[:, 0:1024]; p_tm = ygT[:, 1024:1280]
        dtr, e1, dtt, dta, acum, alast, ecd, dsv = [sm[:, i * 32:(i + 1) * 32] for i in range(8)]
        vals = tk[:, 0:256]; wk = tk[:, 256:384]; wk2 = tk[:, 384:640]; best = tk[:, 640:768]
        paf = tk[:, 768:896]; pbf = tk[:, 896:1024]; i1f = tk[:, 1024:1280]; s1sel = tk[:, 1280:1408]; s2sel = tk[:, 1408:1536]
        eidf = tk[:, 1536:1664]; gexp = tk[:, 1664:1792]; gates = tk[:, 1792:1920]; gsm = tk[:, 1920:1936]; apre = tk[:, 1936:2048]
        idxs = tki[:, 0:256]; pos = tki[:, 256:384]; pa = tki[:, 384:512]; pb = tki[:, 512:640]; eid = tki[:, 640:768]
        wts = sb("wts", [128, 128]); gact = sb("gact", [128, 128]); apre = sb("apre", [128, 128])
        cand = ygT[:, 1024:1280]; oh = ygT[:, 1280:1536]; oh2 = ygT[:, 1536:1792]

        def bc(ap_, shape):
            return ap_.to_broadcast(shape)

        def v3(ap_, a, b):
            return ap_.rearrange("p (a b) -> p a b", a=a, b=b)

        def rawAP(t, offset_elems, dims):
            base = t[:, :]
            pstride = base.ap[0][0]
            return bass.AP(base.tensor, base.offset + offset_elems, [[pstride, 128]] + [[s, c] for s, c in dims])

        plan = []
        for ci in range(nchunks):
            for c0 in range(0, 8192, 512):
                plan.append(("win", w_in, 0, 8, c0, 512))
            plan.append(("win", w_in, 0, 8, 8192, 32))
            for c0 in range(8224, 10272, 512):
                plan.append(("win", w_in, 0, 8, c0, 512))
            for c0 in (0, 512):
                plan.append(("gm", w_gm, 0, 8, c0, 512))
            for c0 in (0, 512):
                plan.append(("sso", w_sso, 0, 8, c0, 512))
                plan.append(("sso", w_sso, 1024, 8, c0, 512))
            for c0 in (0, 512):
                plan.append(("wo", w_o, 0, 8, c0, 512))
            for c0 in range(0, 2048, 512):
                plan.append(("wq", w_q, 0, 8, c0, 512))
            for c0 in (0, 512):
                plan.append(("pg", w_pg, 0, 8, c0, 512))
            plan.append(("pp", w_pp, 0, 2, 0, 1024))
        wstate = {"issued": 0, "next": 0}

        NU = len(plan) // nchunks
        wscr = nc.dram_tensor("wscr", [NU, 128, 4096], BF16, kind="Internal").ap()

        def w_convert():
            for n in range(NU):
                tag, W, k0, nk, c0, ncol = plan[n]
                t = wr[n % NW]
                dst = t[:, 0:nk * ncol].rearrange("p (k n) -> p k n", k=nk)
                src = W[k0:k0 + nk * 128, c0:c0 + ncol].rearrange("(k p) n -> p k n", p=128)
                P.dma("pool", dst, src, writes=["wr%d" % (n % NW)])
                P.dma("sp", wscr[n, :, 0:nk * ncol], t[:, 0:nk * ncol], reads=["wr%d" % (n % NW)], writes=["wscr%d" % n])

        def w_issue(upto):
            while wstate["issued"] <= min(upto, len(plan) - 1):
                n = wstate["issued"]
                tag, W, k0, nk, c0, ncol = plan[n]
                t = wr[n % NW]
                u = n % NU
                P.dma("sp", t[:, 0:nk * ncol], wscr[u, :, 0:nk * ncol], reads=["wscr%d" % u], writes=["wr%d" % (n % NW)])
                wstate["issued"] += 1

        def w_next(tag, hold=0):
            n = wstate["next"]
            wstate["next"] += 1
            assert plan[n][0] == tag, (plan[n][0], tag)
            w_issue(n + NW - 1 - hold)
            nk, ncol = plan[n][3], plan[n][5]
            t = wr[n % NW]
            return t[:, 0:nk * ncol].rearrange("p (k n) -> p k n", k=nk), "wr%d" % (n % NW)

        def mm(out_, lhsT, rhs, start, stop, r, w):
            P.op("pe", lambda e: e.matmul(out_, lhsT, rhs, start=start, stop=stop), reads=r, writes=w)

        def tr(out_, in_, r, w):
            P.op("pe", lambda e: e.transpose(out_, in_, C("ident")), reads=list(r) + ["cps"], writes=w)

        def actf(out_, in_, func, r, w, **kw):
            P.op("act", lambda e: e.activation(out_, in_, func, **kw), reads=r, writes=w)

        def tt(eng, out_, in0, in1, op, r, w):
            P.op(eng, lambda e: e.tensor_tensor(out_, in0, in1, op), reads=r, writes=w)

        def ts(eng, out_, in0, s1, s2, op0, op1, r, w):
            if op1 is None:
                P.op(eng, lambda e: e.tensor_scalar(out_, in0, s1, None, op0), reads=r, writes=w)
            else:
                P.op(eng, lambda e: e.tensor_scalar(out_, in0, s1, s2, op0, op1), reads=r, writes=w)

        def stt(out_, in0, scalar, in1, op0, op1, r, w, accum_out=None):
            if accum_out is None:
                P.op("dve", lambda e: e.scalar_tensor_tensor(out_, in0, scalar, in1, op0, op1), reads=r, writes=w)
            else:
                P.op("dve", lambda e: e.scalar_tensor_tensor(out_, in0, scalar, in1, op0, op1, accum_out=accum_out), reads=r, writes=w)

        def cpy(eng, out_, in_, r, w):
            if eng == "act":
                P.op("act", lambda e: e.copy(out_, in_), reads=r, writes=w)
            else:
                P.op(eng, lambda e: e.tensor_copy(out_, in_), reads=r, writes=w)

        def layer_norm(src, dst, tmp, gname, bname, rsrc, rdst, rtmp, lnst=lnst, rl="lnst"):
            for h_ in range(2):
                P.op("dve", lambda e, h_=h_: e.bn_stats(lnst[:, h_ * 6:(h_ + 1) * 6], src[:, h_ * 512:(h_ + 1) * 512]),
                     reads=[rsrc], writes=[rl])
            P.op("dve", lambda e: e.bn_aggr(lnst[:, 12:14], lnst[:, 0:12]), reads=[rl], writes=[rl])
            ts("dve", lnst[:, 14:15], lnst[:, 13:14], LN_EPS, None, ALU.add, None, [rl], [rl])
            P.op("act", lambda e: e.sqrt(lnst[:, 15:16], lnst[:, 14:15]), reads=[rl], writes=[rl])
            P.op("dve", lambda e: e.reciprocal(lnst[:, 16:17], lnst[:, 15:16]), reads=[rl], writes=[rl])
            ts("dve", tmp, src, lnst[:, 12:13], lnst[:, 16:17], ALU.subtract, ALU.mult, [rsrc, rl], [rtmp])
            tt("dve", tmp, tmp, C(gname), ALU.mult, [rtmp, "cps"], [rtmp])
            tt("dve", dst, tmp, C(bname), ALU.add, [rtmp, "cps"], [rdst])

        taps_done = []

        def tap(name, src_ap, res):
            if name in tapo and name not in tapped:
                tapped.add(name)
                taps_done.append(P.dma("pool", tapo[name], src_ap, reads=[res]))

        tapped = set()

        def stage(name):
            pass

        P.dma("sp", cps[:, :], cp, writes=["cps"])
        P.dma("pool", kT[:, :], kT_in, writes=["kT"])
        P.dma("sp", vv[:, 0:1024], wsT_in, writes=["vv"])
        tt("dve", v3(wsT[:, :], 8, 128), v3(vv[:, 0:1024], 8, 128),
           bc(C("tri").rearrange("p (o t) -> p o t", o=1), [128, 8, 128]), ALU.mult, ["vv", "cps"], ["wsT"])
        actf(a_bc[:, :], C("alog"), AF.Exp, ["cps"], ["a_bc"])
        ts("dve", a_bc[:, :], a_bc[:, :], -1.0, None, ALU.mult, None, ["a_bc"], ["a_bc"])

        cpy("dve", identb[:, :], C("ident"), ["cps"], ["identb"])
        w_convert()
        uscr = nc.dram_tensor("uscr", [16384, 1024], BF16, kind="Internal").ap()
        vscr = nc.dram_tensor("vscr", [16384, 1024], BF16, kind="Internal").ap()
        stg = [(xdt, "xdt"), (HTb, "HTb"), (ynT, "ynT"), (gT, "gT"), (szT, "szT"), (BCb, "BCb")]
        tab_stores = []
        n_ = 0
        for src_t, dst_t in ((peer_u, uscr), (peer_v, vscr)):
            s3 = src_t.rearrange("(p j) d -> p j d", j=128)
            d3 = dst_t.rearrange("(p j) d -> p j d", j=128)
            for sl in range(64):
                tl, tr_ = stg[n_ % len(stg)]
                n_ += 1
                P.dma("pool", tl[:, :].rearrange("p (j d) -> p j d", j=2), s3[:, 2 * sl:2 * sl + 2, :], writes=[tr_])
                tab_stores.append(P.dma("sp", d3[:, 2 * sl:2 * sl + 2, :], tl[:, :].rearrange("p (j d) -> p j d", j=2), reads=[tr_]))
        P.barrier_wait("pool", tab_stores)
        out_dmas = []

        a_split = [0]

        def phaseA(ci):
          if True:
              x1_tm = x1s[ci % NX1]; x1r = "x1_tm%d" % (ci % NX1)
              c_in_seq = ci % nch
              t0 = ci * 128
              if c_in_seq == 0:
                  P.op("pool", lambda e: e.memset(H[:, :], 0.0), writes=["H"])
                  P.op("pool", lambda e: e.memset(HTb[:, :], 0.0), writes=["HTb"])
                  P.op("pool", lambda e: e.memset(halo[:, :], 0.0), writes=["halo"])
              P.dma("sp", x_tm[:, :], x[t0:t0 + 128, :], writes=["ygT"])
              P.dma("sp", p_tm[:, :], p[t0:t0 + 128, :], writes=["ygT"])
              for h_ in range(2):
                  b = nb()
                  for j in range(4):
                      tr(ps[b][:, blk(j)], x_tm[:, blk(h_ * 4 + j)], ["ygT"], ["ps%d" % b])
                  cpy("act" if h_ else "dve", xT[:, h_ * 512:(h_ + 1) * 512], ps[b][:, :], ["ps%d" % b], ["xT"])
              b = nb()
              for j in range(2):
                  tr(ps[b][:, blk(j)], p_tm[:, blk(j)], ["ygT"], ["ps%d" % b])
              cpy("dve", pT[:, :], ps[b][:, 0:256], ["ps%d" % b], ["pT"])

              for un in range(2):
                  wv, wres = w_next("win")
                  b = nb()
                  for j in range(4):
                      for kc in range(8):
                          mm(ps[b][:, blk(j)], wv[:, kc, blk(j)], xT[:, blk(kc)], kc == 0, kc == 7, [wres, "xT"], ["ps%d" % b])
                  actf(uT[:, un * 512:(un + 1) * 512], ps[b][:, :], AF.Gelu_apprx_tanh, ["ps%d" % b], ["uT"])
              for un in range(2):
                  wv, wres = w_next("win")
                  b = nb()
                  for kc in range(8):
                      mm(ps[b][:, :], xT[:, blk(kc)], wv[:, kc, :], kc == 0, kc == 7, [wres, "xT"], ["ps%d" % b])
                  actf(vv[:, un * 512:(un + 1) * 512], ps[b][:, :], AF.Gelu_apprx_tanh, ["ps%d" % b], ["vv"])
              for un in range(4):
                  wv, wres = w_next("win")
                  b = nb()
                  for j in range(4):
                      for kc in range(8):
                          mm(ps[b][:, blk(j)], wv[:, kc, blk(j)], xT[:, blk(kc)], kc == 0, kc == 7, [wres, "xT"], ["ps%d" % b])
                  actf(szT[:, un * 512:(un + 1) * 512], ps[b][:, :], AF.Silu, ["ps%d" % b], ["szT"])
              for un in range(8):
                  wv, wres = w_next("win")
                  b = nb()
                  for j in range(4):
                      for kc in range(8):
                          mm(ps[b][:, blk(j)], wv[:, kc, blk(j)], xT[:, blk(kc)], kc == 0, kc == 7, [wres, "xT"], ["ps%d" % b])
                  rt = rawt[un % 2]
                  rres = "rawt%d" % (un % 2)
                  r3 = v3(rt[:, :], 4, 131)
                  cpy("act", r3[:, :, 0:3], v3(halo[:, un * 12:(un + 1) * 12], 4, 3), ["halo"], [rres])
                  cpy("act", r3[:, :, 3:131], v3(ps[b][:, :], 4, 128), ["ps%d" % b], [rres])
                  cpy("act", v3(halo[:, un * 12:(un + 1) * 12], 4, 3), r3[:, :, 128:131], [rres], ["halo"])
                  cc = cconv[un % 2]
                  ccr = ["cconv%d_%d" % (un % 2, j) for j in range(4)]
                  for j in range(4):
                      jj = un * 4 + j
                      cw = lambda k, jj=jj: C("convw", jj * 4 + k, jj * 4 + k + 1)
                      actf(cc[:, blk(j)], rt[:, j * 131 + 3:j * 131 + 131], AF.Identity, [rres, "cps"], [ccr[j]],
                           bias=C("convb", jj, jj + 1), scale=cw(3))
                  for k in (2, 1, 0):
                      for j in range(4):
                          jj = un * 4 + j
                          cw = lambda k, jj=jj: C("convw", jj * 4 + k, jj * 4 + k + 1)
                          stt(cc[:, blk(j)], rt[:, j * 131 + k:j * 131 + k + 128], cw(k), cc[:, blk(j)], ALU.mult, ALU.add,
                              [rres, "cps", ccr[j]], [ccr[j]])
                  if un < 6:
                      qd = xq[un // 2]
                      qres = "xq%d" % (un // 2)
                      actf(qd[:, (un % 2) * 512:(un % 2 + 1) * 512], cc[:, :], AF.Silu, ccr, [qres])
                      if un >= 4:
                          cpy("act", BCb[:, (un - 4) * 512:(un - 3) * 512], qd[:, (un % 2) * 512:(un % 2 + 1) * 512], [qres], ["BCb"])
                  else:
                      actf(BCb[:, (un - 4) * 512:(un - 3) * 512], cc[:, :], AF.Silu, ccr, ["BCb"])
              wv, wres = w_next("win")
              b = nb()
              for kc in range(8):
                  mm(ps[b][:, 0:32], xT[:, blk(kc)], wv[:, kc, :], kc == 0, kc == 7, [wres, "xT"], ["ps%d" % b])
              tt("dve", dtr, ps[b][:, 0:32], C("dtb"), ALU.add, ["ps%d" % b, "cps"], ["sm_dt"])
              actf(e1, dtr, AF.Exp, ["sm_dt"], ["sm_dt"])
              actf(dtt, e1, AF.Ln, ["sm_dt"], ["sm_dt"], bias=C("ones", 0, 1), scale=1.0)
              tt("dve", dta, dtt, a_bc[:, :], ALU.mult, ["sm_dt", "a_bc"], ["sm_dt"])
              for un in range(4):
                  wv, wres = w_next("win")
                  b = nb()
                  for j in range(4):
                      for kc in range(8):
                          mm(ps[b][:, blk(j)], wv[:, kc, blk(j)], xT[:, blk(kc)], kc == 0, kc == 7, [wres, "xT"], ["ps%d" % b])
                  for j in range(4):
                      jj = un * 4 + j
                      actf(gT[:, blk(jj)], ps[b][:, blk(j)], AF.Sigmoid, ["ps%d" % b, "cps"], ["gT"],
                           bias=C("bgate", jj, jj + 1), scale=1.0)

              tap("uT", uT[:, :], "uT"); tap("v", vv[:, 0:1024], "vv"); tap("gT", gT[:, :], "gT"); tap("szT", szT[:, :], "szT")
              tap("xq0", xq[0][:, :], "xq0"); tap("dt", dtt, "sm_dt")
              stage("win")
              layer_norm(v_tm, vnb[:, :], vh, "gmg", "gmb", "vv", "vnb", "vv")
              for h_ in range(2):
                  b = nb()
                  for j in range(4):
                      g = h_ * 4 + j
                      mm(ps[b][:, blk(j)], vnb[:, blk(g)], wsT[:, blk(g)], True, True, ["vnb", "wsT"], ["ps%d" % b])
                  tt("dve", cacc[:, :], ps[b][:, :], C("bs", h_ * 512, (h_ + 1) * 512), ALU.add, ["ps%d" % b, "cps"], ["cacc"])
                  tt("dve", gaT[:, h_ * 512:(h_ + 1) * 512], cacc[:, :], uT[:, h_ * 512:(h_ + 1) * 512], ALU.mult, ["cacc", "uT"], ["gaT"])
              for h_ in range(2):
                  wv, wres = w_next("gm")
                  b = nb()
                  for j in range(4):
                      for kc in range(8):
                          mm(ps[b][:, blk(j)], wv[:, kc, blk(j)], gaT[:, blk(kc)], kc == 0, kc == 7, [wres, "gaT"], ["ps%d" % b])
                  tt("dve", mA[:, h_ * 512:(h_ + 1) * 512], ps[b][:, :], gT[:, h_ * 512:(h_ + 1) * 512], ALU.mult, ["ps%d" % b, "gT"], ["mA"])

              tap("mA", mA[:, :], "mA")
              stage("gmlp")
              b = nb()
              mm(ps[b][:, 0:32], C("tri"), dta, True, True, ["cps", "sm_dt"], ["ps%d" % b])
              cpy("dve", acum, ps[b][:, 0:32], ["ps%d" % b], ["sm_acum"])
              for qd_i in range(2):
                  for h_ in range(2):
                      b = nb()
                      for j in range(4):
                          tr(ps[b][:, blk(j)], xq[qd_i][:, blk(h_ * 4 + j)], ["xq%d" % qd_i], ["ps%d" % b])
                      hs = (qd_i * 2 + h_) * 8
                      tt("dve", v3(xdt[:, hs * 64:(hs + 8) * 64], 8, 64), v3(ps[b][:, :], 8, 64),
                         bc(dtt[:, hs:hs + 8].rearrange("p (h o) -> p h o", o=1), [128, 8, 64]), ALU.mult, ["ps%d" % b, "sm_dt"], ["xdt"])
              for h_ in range(2):
                  b = nb()
                  for j in range(4):
                      g = h_ * 4 + j
                      mm(ps[b][:, blk(j)], BCb[:, blk(g)], BCb[:, blk(8 + g)], True, True, ["BCb"], ["ps%d" % b])
                  cpy("act", cbT[:, h_ * 512:(h_ + 1) * 512], ps[b][:, :], ["ps%d" % b], ["cbT"])
              stage("ssd_a")
              for g in range(8):
                  cpy("act", v3(drep, 4, 128), bc(dta[:, 4 * g:4 * g + 4].rearrange("p (h o) -> p h o", o=1), [128, 4, 128]), ["sm_dt"], ["sd_drep"])
                  ba = nb()
                  for hh in range(4):
                      mm(ps[ba][:, blk(hh)], drep[:, blk(hh)], C("tri"), True, True, ["sd_drep", "cps"], ["ps%d" % ba])
                  actf(Ebc, ps[ba][:, :], AF.Exp, ["ps%d" % ba], ["sd_Ebc"])
                  cpy("dve", alast[:, 4 * g:4 * g + 4], rawAP(ps[ba], 127, [(128, 4)]), ["ps%d" % ba], ["sm_alast"])
                  cpy("act", ecd[:, 4 * g:4 * g + 4], rawAP(sd, 512 + 127, [(128, 4)]), ["sd_Ebc"], ["sm_ecd"])
                  for hh in range(4):
                      h = 4 * g + hh
                      stt(segm[:, blk(hh)], ps[ba][:, blk(hh)], acum[:, h:h + 1], C("negm"), ALU.subtract, ALU.add,
                          ["ps%d" % ba, "sm_acum", "cps"], ["sd_segm"])
                  actf(tmpL, segm, AF.Exp, ["sd_segm"], ["sd_tmpL"])
                  tt("dve", v3(MT[:, :], 4, 128), v3(tmpL, 4, 128),
                     bc(cbT[:, blk(g)].rearrange("p (o t) -> p o t", o=1), [128, 4, 128]), ALU.mult, ["sd_tmpL", "cbT"], ["MT"])
                  tt("dve", v3(CsT[:, :], 4, 128), bc(BCb[:, blk(8 + g)].rearrange("p (o t) -> p o t", o=1), [128, 4, 128]),
                     v3(Ebc, 4, 128), ALU.mult, ["BCb", "sd_Ebc"], ["CsT"])
                  for q in range(2):
                      j = g * 2 + q
                      by = nb()
                      for e_ in range(2):
                          hh = 2 * q + e_
                          h = 4 * g + hh
                          o_ = ps[by][e_ * 64:(e_ + 1) * 64, 0:128]
                          mm(o_, xdt[:, h * 64:(h + 1) * 64], MT[:, blk(hh)], True, False, ["xdt", "MT"], ["ps%d" % by])
                          mm(o_, HTb[:, h * 64:(h + 1) * 64], CsT[:, blk(hh)], False, True, ["HTb", "CsT"], ["ps%d" % by])
                      xs_blk = xq[j // 8][:, blk(j % 8)]
                      yt = ytmp[j % 2]
                      stt(yt[:, :], xs_blk, C("dsk", j, j + 1), ps[by][:, 0:128], ALU.mult, ALU.add,
                          ["xq%d" % (j // 8), "cps", "ps%d" % by], ["ytmp%d" % (j % 2)])
                      tt("dve", ygT[:, blk(j)], yt[:, :], szT[:, blk(j)], ALU.mult, ["ytmp%d" % (j % 2), "szT"], ["ygT"])
                      actf(ysq[j % 2][:, :], ygT[:, blk(j)], AF.Square, ["ygT"], ["ysq%d" % (j % 2)])
                      mm(ps[7][:, 0:128], C("ones"), ysq[j % 2][:, :], j == 0, j == 15, ["cps", "ysq%d" % (j % 2)], ["ps7"])
              tap("ygT", ygT[:, :], "ygT")
              stage("ssd_g")
              ts("dve", rstd[:, :], ps[7][:, 0:128], 1.0 / 2048.0, LN_EPS, ALU.mult, ALU.add, ["ps7"], ["rstd"])
              P.op("act", lambda e: e.sqrt(rstd[:, :], rstd[:, :]), reads=["rstd"], writes=["rstd"])
              P.op("dve", lambda e: e.reciprocal(rstd[:, :], rstd[:, :]), reads=["rstd"], writes=["rstd"])
              for j in range(16):
                  stt(ynT[:, blk(j)], ygT[:, blk(j)], C("nw", j, j + 1), rstd[:, :], ALU.mult, ALU.mult, ["ygT", "cps", "rstd"], ["ynT"])
              for h_ in range(2):
                  wv0, wres0 = w_next("sso")
                  wv1, wres1 = w_next("sso", hold=1)
                  b = nb()
                  for j in range(4):
                      for kc in range(16):
                          wv, wres = (wv0, wres0) if kc < 8 else (wv1, wres1)
                          mm(ps[b][:, blk(j)], wv[:, kc % 8, blk(j)], ynT[:, blk(kc)], kc == 0, kc == 15, [wres, "ynT"], ["ps%d" % b])
                  tt("dve", cacc[:, :], ps[b][:, :], gT[:, 1024 + h_ * 512:1024 + (h_ + 1) * 512], ALU.mult, ["ps%d" % b, "gT"], ["cacc"])
                  tt("dve", mT[:, h_ * 512:(h_ + 1) * 512], cacc[:, :], mA[:, h_ * 512:(h_ + 1) * 512], ALU.add, ["cacc", "mA"], ["mT"])
              tap("mT", mT[:, :], "mT")
              stage("ssd_n")
              for h_ in range(2):
                  b = nb()
                  for j in range(4):
                      tr(ps[b][:, blk(j)], xq[2][:, blk(h_ * 4 + j)], ["xq2"], ["ps%d" % b])
                  cpy("act", B_tm[:, h_ * 512:(h_ + 1) * 512], ps[b][:, :], ["ps%d" % b], ["B_tm"])
              tt("dve", dsv, alast, acum, ALU.subtract, ["sm_alast", "sm_acum"], ["sm_dsv"])
              actf(dsv, dsv, AF.Exp, ["sm_dsv"], ["sm_dsv"])
              tt("dve", v3(xdtds[:, :], 32, 64), v3(xdt[:, :], 32, 64),
                 bc(dsv.rearrange("p (h o) -> p h o", o=1), [128, 32, 64]), ALU.mult, ["xdt", "sm_dsv"], ["ynT"])
              for gp in range(4):
                  b = nb()
                  for e_ in range(2):
                      g = gp * 2 + e_
                      mm(ps[b][:, e_ * 256:(e_ + 1) * 256], B_tm[:, blk(g)], xdtds[:, g * 256:(g + 1) * 256], True, True,
                         ["B_tm", "ynT"], ["ps%d" % b])
                  hs = gp * 8
                  tt("dve", v3(H[:, gp * 512:(gp + 1) * 512], 8, 64), v3(H[:, gp * 512:(gp + 1) * 512], 8, 64),
                     bc(ecd[:, hs:hs + 8].rearrange("p (h o) -> p h o", o=1), [128, 8, 64]), ALU.mult, ["H", "sm_ecd"], ["H"])
                  tt("dve", H[:, gp * 512:(gp + 1) * 512], H[:, gp * 512:(gp + 1) * 512], ps[b][:, :], ALU.add, ["H", "ps%d" % b], ["H"])
                  cpy("act", HTb[:, gp * 512:(gp + 1) * 512], H[:, gp * 512:(gp + 1) * 512], ["H"], ["HTb"])

              tap("ygT", ygT[:, :], "ygT"); tap("mT", mT[:, :], "mT"); tap("H", H[:, :], "H")
              stage("ssd")
              x1pre = ygT[:, 0:1024]
              P.dma("sp", x1pre, x[t0:t0 + 128, :], writes=["ygT"])
              for h_ in range(2):
                  wv, wres = w_next("wo")
                  b = nb()
                  for kc in range(8):
                      mm(ps[b][:, :], mT[:, blk(kc)], wv[:, kc, :], kc == 0, kc == 7, [wres, "mT"], ["ps%d" % b])
                  stt(x1pre[:, h_ * 512:(h_ + 1) * 512], x1pre[:, h_ * 512:(h_ + 1) * 512], ALPHA, ps[b][:, :], ALU.mult, ALU.add,
                      ["ygT", "ps%d" % b], ["ygT"])
              layer_norm(x1pre, x1_tm[:, :], x1pre, "ln1g", "ln1b", "ygT", x1r, "ygT")
              tap("x1", x1_tm[:, :], x1r)
              for h_ in range(2):
                  b = nb()
                  for j in range(4):
                      tr(ps[b][:, blk(j)], x1_tm[:, blk(h_ * 4 + j)], [x1r], ["ps%d" % b])
                  cpy("act" if h_ else "dve", x1T[:, h_ * 512:(h_ + 1) * 512], ps[b][:, :], ["ps%d" % b], ["x1T"])

              stage("ln1")
              for un in range(4):
                  wv, wres = w_next("wq")
                  b = nb()
                  for j in range(4):
                      for kc in range(8):
                          mm(ps[b][:, blk(j)], wv[:, kc, blk(j)], x1T[:, blk(kc)], kc == 0, kc == 7, [wres, "x1T"], ["ps%d" % b])
                  cpy("act", qT[:, un * 512:(un + 1) * 512], ps[b][:, :], ["ps%d" % b], ["xdt"])
              for bi in range(4):
                  b = nb()
                  for j in range(4):
                      i = bi * 4 + j
                      mm(ps[b][:, blk(j)], qT[:, blk(i)], kT[:, blk(i)], True, True, ["xdt", "kT"], ["ps%d" % b])
                  for j in range(4):
                      i = bi * 4 + j
                      s_i = ps[b][:, blk(j)]
                      r_ = ["ps%d" % b]
                      P.op("dve", lambda e, s_i=s_i, i=i: e.max(vals[:, i * 16:i * 16 + 8], s_i), reads=r_, writes=["tk"])
                      P.op("dve", lambda e, s_i=s_i, i=i: e.match_replace(wk, vals[:, i * 16:i * 16 + 8], s_i, -1e30), reads=r_ + ["tk"], writes=["tk"])
                      P.op("dve", lambda e, i=i: e.max(vals[:, i * 16 + 8:i * 16 + 16], wk), reads=["tk"], writes=["tk"])
                      P.op("dve", lambda e, s_i=s_i, i=i: e.max_index(idxs[:, i * 16:i * 16 + 8], vals[:, i * 16:i * 16 + 8], s_i), reads=r_ + ["tk"], writes=["tki"])
                      P.op("dve", lambda e, s_i=s_i, i=i: e.max_index(idxs[:, i * 16 + 8:i * 16 + 16], vals[:, i * 16 + 8:i * 16 + 16], s_i), reads=r_ + ["tk"], writes=["tki"])
              cpy("dve", i1f, idxs, ["tki"], ["tk"])
              for h in range(8):
                  tt("dve", v3(cand, 16, 16),
                     bc(vals[:, (2 * h) * 16:(2 * h) * 16 + 16].rearrange("p (a o) -> p a o", o=1), [128, 16, 16]),
                     bc(vals[:, (2 * h + 1) * 16:(2 * h + 1) * 16 + 16].rearrange("p (o b) -> p o b", o=1), [128, 16, 16]),
                     ALU.add, ["tk"], ["ygT"])
                  bh = best[:, h * 16:(h + 1) * 16]
                  ph = pos[:, h * 16:(h + 1) * 16]
                  P.op("dve", lambda e, bh=bh: e.max(bh[:, 0:8], cand), reads=["ygT"], writes=["tk"])
                  P.op("dve", lambda e, bh=bh: e.match_replace(wk2, bh[:, 0:8], cand, -1e30), reads=["ygT", "tk"], writes=["tk"])
                  P.op("dve", lambda e, bh=bh: e.max(bh[:, 8:16], wk2), reads=["tk"], writes=["tk"])
                  P.op("dve", lambda e, bh=bh, ph=ph: e.max_index(ph[:, 0:8], bh[:, 0:8], cand), reads=["ygT", "tk"], writes=["tki"])
                  P.op("dve", lambda e, bh=bh, ph=ph: e.max_index(ph[:, 8:16], bh[:, 8:16], cand), reads=["ygT", "tk"], writes=["tki"])
              P.op("dve", lambda e: e.tensor_single_scalar(pa, pos, 4, ALU.logical_shift_right), reads=["tki"], writes=["tki"])
              P.op("dve", lambda e: e.tensor_single_scalar(pb, pos, 15, ALU.bitwise_and), reads=["tki"], writes=["tki"])
              cpy("dve", paf, pa, ["tki"], ["tk"])
              cpy("dve", pbf, pb, ["tki"], ["tk"])
              for h in range(8):
                  for which, pf, sel in ((0, paf, s1sel), (1, pbf, s2sel)):
                      tt("dve", v3(oh, 16, 16),
                         bc(C("iota16").rearrange("p (o a) -> p o a", o=1), [128, 16, 16]),
                         bc(pf[:, h * 16:(h + 1) * 16].rearrange("p (r o) -> p r o", o=1), [128, 16, 16]),
                         ALU.is_equal, ["cps", "tk"], ["ygT"])
                      ioff = (2 * h + which) * 16
                      tt("dve", v3(oh2, 16, 16), v3(oh, 16, 16),
                         bc(i1f[:, ioff:ioff + 16].rearrange("p (o a) -> p o a", o=1), [128, 16, 16]),
                         ALU.mult, ["ygT", "tk"], ["ygT"])
                      P.op("dve", lambda e, sel=sel, h=h: e.tensor_reduce(sel[:, h * 16:(h + 1) * 16], v3(oh2, 16, 16), AX.X, ALU.add),
                           reads=["ygT"], writes=["tk"])
              stt(eidf, s1sel, 128.0, s2sel, ALU.mult, ALU.add, ["tk"], ["tk"])
              cpy("dve", eid, eidf, ["tk"], ["tki"])
              tt("dve", gexp.rearrange("p (h r) -> p h r", h=8), best.rearrange("p (h r) -> p h r", h=8),
                 bc(rawAP(tk, 640, [(16, 8), (0, 1)]), [128, 8, 16]) if False else bc(best.rearrange("p (h r) -> p h r", h=8)[:, :, 0:1], [128, 8, 16]),
                 ALU.subtract, ["tk"], ["tk"])
              actf(gexp, gexp, AF.Exp, ["tk"], ["tk"])
              P.op("dve", lambda e: e.tensor_reduce(gsm[:, 0:8], gexp.rearrange("p (h r) -> p h r", h=8), AX.X, ALU.add), reads=["tk"], writes=["tk"])
              P.op("dve", lambda e: e.reciprocal(gsm[:, 8:16], gsm[:, 0:8]), reads=["tk"], writes=["tk"])
              tt("dve", gates.rearrange("p (h r) -> p h r", h=8), gexp.rearrange("p (h r) -> p h r", h=8),
                 bc(gsm[:, 8:16].rearrange("p (h o) -> p h o", o=1), [128, 8, 16]), ALU.mult, ["tk"], ["tk"])
              tap("eid", eidf, "tk")
              tap("gates", gates, "tk")
              par = ci % 2
              cpy("dve", eid_p[par][:, :], eid, ["tki"], ["eid_p%d" % par])
              cpy("dve", gates_p[par][:, :], gates, ["tk"], ["gates_p%d" % par])
              a_split[0] = len(P.cur)
              pg = ygT[:, 0:1024]
              for h_ in range(2):
                  wv, wres = w_next("pg")
                  b = nb()
                  for kc in range(8):
                      mm(ps[b][:, :], x1T[:, blk(kc)], wv[:, kc, :], kc == 0, kc == 7, [wres, "x1T"], ["ps%d" % b])
                  tt("dve", pg[:, h_ * 512:(h_ + 1) * 512], ps[b][:, :], C("pleb", h_ * 512, (h_ + 1) * 512), ALU.add, ["ps%d" % b, "cps"], ["ygT"])
              actf(pg, pg, AF.Sigmoid, ["ygT"], ["ygT"])
              wv, wres = w_next("pp")
              for h_ in range(2):
                  b = nb()
                  for kc in range(2):
                      mm(ps[b][:, :], pT[:, blk(kc)], wv[:, kc, h_ * 512:(h_ + 1) * 512], kc == 0, kc == 1, [wres, "pT"], ["ps%d" % b])
                  tt("dve", ple_s[:, h_ * 512:(h_ + 1) * 512], pg[:, h_ * 512:(h_ + 1) * 512], ps[b][:, :], ALU.mult, ["ygT", "ps%d" % b], ["ple_s"])

        def phaseB(ci):
          if True:
              x1_tm = x1s[ci % NX1]; x1r = "x1_tm%d" % (ci % NX1)
              par = ci % 2
              t0 = ci * 128
              eidc = eid_p[par]; er = "eid_p%d" % par
              gatc = gates_p[par]; gr = "gates_p%d" % par
              NG = NGB
              for k in range(128):
                  g_ = gbt[k % NG][:, :]
                  gres = "gb%d" % (k % NG)
                  P.op("pool", lambda e, g_=g_, k=k: e.indirect_dma_start(
                      out=g_, out_offset=None, in_=uscr,
                      in_offset=bass.IndirectOffsetOnAxis(ap=eidc[:, k:k + 1], axis=0)),
                      reads=[er], writes=[gres], dma=True)
                  stt(junk[:, :], g_, 1.0, x1_tm[:, :], ALU.mult, ALU.mult, [gres, x1r], ["junk", "apre"], accum_out=apre[:, k:k + 1])
              actf(gact[:, :], apre[:, :], AF.Gelu_apprx_tanh, ["apre"], ["gact"])
              tt("dve", wts[:, :], gact[:, :], gatc[:, :], ALU.mult, ["gact", gr], ["wts"])
              for k in range(128):
                  g_ = gbt[k % NG][:, :]
                  gres = "gb%d" % (k % NG)
                  P.op("pool", lambda e, g_=g_, k=k: e.indirect_dma_start(
                      out=g_, out_offset=None, in_=vscr,
                      in_offset=bass.IndirectOffsetOnAxis(ap=eidc[:, k:k + 1], axis=0)),
                      reads=[er], writes=[gres], dma=True)
                  dg = dgt[k % 4]
                  dres = "dg%d" % (k % 4)
                  actf(dg[:, :], identb[:, :], AF.Copy, ["identb", "wts"], [dres], scale=wts[:, k:k + 1])
                  mm(ps[5][:, :], dg[:, :], g_[:, 0:512], k == 0, k == 127, [dres, gres], ["ps5"])
                  mm(ps[6][:, :], dg[:, :], g_[:, 512:1024], k == 0, k == 127, [dres, gres], ["ps6"])
              tsum = tsum_t[:, :]
              for h_ in range(2):
                  stt(tsum[:, h_ * 512:(h_ + 1) * 512], x1_tm[:, h_ * 512:(h_ + 1) * 512], ALPHA, ps[5 + h_][:, :], ALU.mult, ALU.add,
                      [x1r, "ps%d" % (5 + h_)], ["tsum"])
              tt("dve", tsum, tsum, ple_s[:, :], ALU.add, ["tsum", "ple_s"], ["tsum"])
              layer_norm(tsum, tsum, tsum, "ln2g", "ln2b", "tsum", "tsum", "tsum", lnst=lnst2, rl="lnst2")
              out_dmas.append(P.dma("sp", out[t0:t0 + 128, :], tsum, reads=["tsum"]))

        def merge(bl, al):
            outl = []
            na, nbl = len(al), len(bl)
            ia = 0
            for ib in range(nbl):
                outl.append(bl[ib])
                tgt = ((ib + 1) * na) // nbl
                outl.extend(al[ia:tgt])
                ia = tgt
            outl.extend(al[ia:])
            return outl

        P.begin(); phaseA(0); P.extend(P.end())
        for ci in range(nchunks):
            P.begin(); phaseB(ci); bl = P.end()
            al, ae = [], []
            if ci + 1 < nchunks:
                P.begin(); phaseA(ci + 1); al = P.end()
                al, ae = al[:a_split[0]], al[a_split[0]:]
            P.extend(merge(bl, al))
            P.extend(ae)

        P.barrier_wait("sp", out_dmas + taps_done)
        P.emit()
    return nc


_NC_CACHE = {}


def kernel(**I):
    I = {k: np.asarray(v) for k, v in I.items()}
    B, S, D = I["x"].shape
    ncores = 8
    nseq = B // ncores
    nch = S // 128
    key = (nseq, nch)
    if key not in _NC_CACHE:
        _NC_CACHE[key] = build(nseq, nch)
    nc = _NC_CACHE[key]
    cp = pack_cp(I)
    wsT = np.ascontiguousarray(I["gm_w_s"][0].transpose(2, 0, 1).reshape(128, 1024))
    k1, k2 = I["peer_k1"][0], I["peer_k2"][0]
    kk = np.stack([k1, k2], axis=1)
    kT = np.ascontiguousarray(kk.transpose(3, 0, 1, 2).reshape(128, 2048))
    shared = {
        "w_in": np.ascontiguousarray(I["w_in"][0]), "gm_w_out": np.ascontiguousarray(I["gm_w_out"][0]),
        "ssd_w_out": np.ascontiguousarray(I["ssd_w_out"][0]), "w_o": np.ascontiguousarray(I["w_o"][0]),
        "w_q": np.ascontiguousarray(I["peer_w_q"][0]), "w_pg": np.ascontiguousarray(I["ple_w_gate"][0]),
        "w_pp": np.ascontiguousarray(I["ple_w_proj"][0]), "peer_u": np.ascontiguousarray(I["peer_u"][0]),
        "peer_v": np.ascontiguousarray(I["peer_v"][0]), "cp": cp, "wsT": wsT, "kT": kT,
    }
    in_maps = []
    for c in range(ncores):
        m = dict(shared)
        m["x"] = np.ascontiguousarray(I["x"][c * nseq:(c + 1) * nseq].reshape(nseq * S, D))
        m["p"] = np.ascontiguousarray(I["p"][0, c * nseq:(c + 1) * nseq].reshape(nseq * S, 256))
        in_maps.append(m)
    res = run_bass_kernel_spmd(nc, in_maps, core_ids=list(range(ncores)))
    outs = [np.asarray(r["out"]).reshape(nseq, S, D) for r in res.results]
    return np.concatenate(outs, axis=0).astype(np.float32)
```

```python
import numpy as np
import concourse.bass as bass
import concourse.mybir as mybir
from concourse.bass_utils import run_bass_kernel_spmd

dt = mybir.dt
F32, BF16, U32, I32 = dt.float32, dt.bfloat16, dt.uint32, dt.int32
AF = mybir.ActivationFunctionType
ALU = mybir.AluOpType
AX = mybir.AxisListType

EPOCH = 6000
NDS = {"sp": 8, "act": 4, "pool": 24}


class Op:
    __slots__ = ("eng", "fn", "reads", "writes", "dma", "deps", "sig", "idx", "waits", "depops")

    def __init__(self, eng, fn, reads, writes, dma):
        self.eng, self.fn, self.reads, self.writes, self.dma = eng, fn, reads, writes, dma
        self.deps = set()
        self.sig = None
        self.waits = []
        self.depops = None
        self.idx = -1


class Prog:
    ENG = ("pe", "act", "dve", "pool", "sp")

    def __init__(self, nc):
        self.nc = nc
        self.ops = []
        self.cur = self.ops
        self._stack = []

    def op(self, eng, fn, reads=(), writes=(), dma=False):
        reads, writes = list(reads), list(writes)
        for r in list(reads):
            if isinstance(r, str) and r.startswith("ps") and r not in writes:
                writes.append(r)
        o = Op(eng, fn, tuple(reads), tuple(writes), dma)
        self.cur.append(o)
        return o

    def begin(self):
        self._stack.append(self.cur)
        self.cur = []

    def end(self):
        l = self.cur
        self.cur = self._stack.pop()
        return l

    def extend(self, lst):
        self.cur.extend(lst)

    def analyze(self):
        last_w, readers = {}, {}
        for i, o in enumerate(self.ops):
            o.idx = i
        for o in self.ops:
            if o.depops is not None:
                o.deps = {d.idx for d in o.depops}
                continue
            for r in o.reads:
                if r in last_w:
                    o.deps.add(last_w[r])
            for w in o.writes:
                if w in last_w:
                    o.deps.add(last_w[w])
                for rd in readers.get(w, ()):
                    o.deps.add(rd)
            for w in o.writes:
                last_w[w] = o.idx
                readers[w] = []
            for r in o.reads:
                if r not in o.writes:
                    readers.setdefault(r, []).append(o.idx)
            o.deps.discard(o.idx)

    def dma(self, q, out, in_, reads=(), writes=(), **kw):
        return self.op(q, lambda e: e.dma_start(out=out, in_=in_, **kw), reads, writes, dma=True)

    def finalize(self):
        self.analyze()
        ops = self.ops
        for o in ops:
            if o.eng == "pe" and not o.dma:
                o.deps = {d for d in o.deps if not (ops[d].eng == "pe" and not ops[d].dma)}
        needed = set()
        for o in ops:
            needed |= o.deps
        cnt = {e: 0 for e in self.ENG}
        dcnt = {e: 0 for e in self.ENG}
        self.sem_keys = set()
        for o in ops:
            if o.dma:
                n = dcnt[o.eng]
                dcnt[o.eng] += 1
                nds = NDS[o.eng]
                key = ("d", o.eng, n % nds)
                o.sig = (key, 16 * (n // nds + 1), 16)
                self.sem_keys.add(key)
                if n >= nds:
                    o.waits.append((key, 16 * (n // nds)))
            elif o.idx in needed:
                cnt[o.eng] += 1
                c = cnt[o.eng]
                key = ("c", o.eng, (c - 1) // EPOCH)
                o.sig = (key, (c - 1) % EPOCH + 1, 1)
                self.sem_keys.add(key)
        for o in ops:
            best = {}
            for d in o.deps:
                key, val, _ = ops[d].sig
                if best.get(key, 0) < val:
                    best[key] = val
            for key, val in o.waits:
                if best.get(key, 0) < val:
                    best[key] = val
            o.waits = sorted(best.items(), key=lambda kv: str(kv[0]))

    def emit(self, final_wait_ops=()):
        from contextlib import ExitStack
        self.finalize()
        nc = self.nc
        ops = self.ops
        with ExitStack() as st:
            sems = {}
            for key in sorted(self.sem_keys, key=str):
                sems[key] = st.enter_context(nc.semaphore("s_%s_%s_%d" % key))
            block = st.enter_context(nc.Block())

            def run(engname):
                def body(e):
                    for o in ops:
                        if o.eng != engname:
                            continue
                        for key, val in o.waits:
                            e.wait_ge(sems[key], val)
                        if o.fn is None:
                            continue
                        inst = o.fn(e)
                        if o.sig is not None:
                            inst.then_inc(sems[o.sig[0]], o.sig[2])
                return body

            block.sync(run("sp"))
            block.scalar(run("act"))
            block.vector(run("dve"))
            block.gpsimd(run("pool"))
            block.tensor(run("pe"))

    def barrier_wait(self, eng, dep_ops):
        o = Op(eng, None, (), (), False)
        o.depops = list(dep_ops)
        self.cur.append(o)
        return o


LN_EPS = 1e-5
ALPHA = 2.0 ** 0.25
NEG = -1.0e5


def cp_layout():
    names = [("ident", 128), ("tri", 128), ("negm", 128), ("ones", 128), ("iota16", 16),
             ("bgate", 16), ("gmg", 1024), ("gmb", 1024), ("bs", 1024), ("convw", 128), ("convb", 32),
             ("dtb", 32), ("alog", 32), ("dsk", 16), ("nw", 16),
             ("ln1g", 1024), ("ln1b", 1024), ("ln2g", 1024), ("ln2b", 1024), ("pleb", 1024)]
    off = {}
    o = 0
    for n, w in names:
        off[n] = (o, w)
        o += w
    return off, o


def pack_cp(I):
    off, ncol = cp_layout()
    cp = np.zeros((128, ncol), np.float32)

    def put(n, a):
        o, w = off[n]
        cp[:, o:o + w] = a

    idx = np.arange(128)
    put("ident", np.eye(128, dtype=np.float32))
    put("tri", (idx[:, None] <= idx[None, :]).astype(np.float32))
    put("negm", np.where(idx[:, None] <= idx[None, :], 0.0, NEG).astype(np.float32))
    put("ones", np.ones((128, 128), np.float32))
    put("iota16", np.broadcast_to(np.arange(16, dtype=np.float32), (128, 16)))
    put("bgate", I["b_gate"][0].reshape(16, 128).T)
    put("gmg", np.broadcast_to(I["gm_ln_g"][0], (128, 1024)))
    put("gmb", np.broadcast_to(I["gm_ln_b"][0], (128, 1024)))
    put("bs", np.broadcast_to(I["gm_b_s"][0].reshape(1024), (128, 1024)))
    put("convw", I["ssd_conv_w"][0].reshape(4, 32, 128).transpose(2, 1, 0).reshape(128, 128))
    put("convb", I["ssd_conv_b"][0].reshape(32, 128).T)
    put("dtb", np.broadcast_to(I["ssd_dt_bias"][0], (128, 32)))
    put("alog", np.broadcast_to(I["ssd_a_log"][0], (128, 32)))
    put("dsk", np.repeat(I["ssd_d"][0], 64).reshape(16, 128).T)
    put("nw", I["ssd_norm_w"][0].reshape(16, 128).T)
    put("ln1g", np.broadcast_to(I["ln1_g"][0], (128, 1024)))
    put("ln1b", np.broadcast_to(I["ln1_b"][0], (128, 1024)))
    put("ln2g", np.broadcast_to(I["ln2_g"][0], (128, 1024)))
    put("ln2b", np.broadcast_to(I["ln2_b"][0], (128, 1024)))
    put("pleb", np.broadcast_to(I["ple_b_gate"][0], (128, 1024)))
    return cp


class StopBuild(Exception):
    pass


def build(nseq, nch, taps=(), stop=None, NX1=2, NGB=8):
    from contextlib import ExitStack
    nchunks = nseq * nch
    ntok = nchunks * 128
    nc = bass.Bass("TRN2", target_bir_lowering=False)
    off, NCP = cp_layout()

    def din(name, shape, d=F32):
        return nc.dram_tensor(name, shape, d, kind="ExternalInput").ap()

    x = din("x", [ntok, 1024]); p = din("p", [ntok, 256])
    w_in = din("w_in", [1024, 10272]); w_gm = din("gm_w_out", [1024, 1024]); w_sso = din("ssd_w_out", [2048, 1024])
    w_o = din("w_o", [1024, 1024]); w_q = din("w_q", [1024, 2048]); w_pg = din("w_pg", [1024, 1024]); w_pp = din("w_pp", [256, 1024])
    peer_u = din("peer_u", [16384, 1024]); peer_v = din("peer_v", [16384, 1024])
    cp = din("cp", [128, NCP]); wsT_in = din("wsT", [128, 1024]); kT_in = din("kT", [128, 2048])
    out = nc.dram_tensor("out", [ntok, 1024], F32, kind="ExternalOutput").ap()
    tapo = {}
    for name, shape in taps:
        tapo[name] = nc.dram_tensor("tap_" + name, list(shape), F32, kind="ExternalOutput").ap()

    with ExitStack() as st:
        def sb(name, shape, d=F32):
            return st.enter_context(nc.sbuf_tensor("sb_" + name, shape, d))

        P = Prog(nc)
        ps = [st.enter_context(nc.psum_tensor("ps%d" % i, [128, 512], F32)) for i in range(8)]
        bank_ctr = [0]

        def nb():
            b = bank_ctr[0] % 5
            bank_ctr[0] += 1
            return b

        def blk(j, w=128):
            return slice(j * w, (j + 1) * w)

        cps = sb("cps", [128, NCP])

        def C(n, a=None, b=None):
            o, w = off[n]
            if a is None:
                return cps[:, o:o + w]
            return cps[:, o + a:o + b]

        wsT = sb("wsT", [128, 1024], BF16)
        kT = sb("kT", [128, 2048], BF16)
        a_bc = sb("a_bc", [128, 32])
        NW = 2
        wr = [sb("wr%d" % i, [128, 4096], BF16) for i in range(NW)]
        xT = sb("xT", [128, 1024], BF16); pT = sb("pT", [128, 256], BF16)
        uT = sb("uT", [128, 1024], BF16)
        vv = sb("vv", [128, 1024])
        vnb = sb("vnb", [128, 1024], BF16)
        gT = sb("gT", [128, 2048], BF16); szT = sb("szT", [128, 2048], BF16)
        rawt = [sb("rawt%d" % i, [128, 4 * 131]) for i in range(2)]
        cacc = sb("cacc", [128, 512])
        cconv = [sb("cconv%d" % i, [128, 512]) for i in range(2)]
        xq = [sb("xq%d" % i, [128, 1024]) for i in range(3)]
        BCb = sb("BCb", [128, 2048], BF16)
        halo = sb("halo", [128, 96])
        gaT = sb("gaT", [128, 1024], BF16)
        mA = sb("mA", [128, 1024]); cbT = sb("cbT", [128, 1024], BF16)
        sm = sb("sm", [128, 256])
        sd = sb("sd", [128, 2048])
        MT = sb("MT", [128, 512], BF16); CsT = sb("CsT", [128, 512], BF16)
        xdt = sb("xdt", [128, 2048], BF16)
        B_tm = sb("B_tm", [128, 1024], BF16)
        ytmp = [sb("ytmp%d" % i, [128, 128]) for i in range(2)]
        ysq = [sb("ysq%d" % i, [128, 128]) for i in range(2)]
        ygT = sb("ygT", [128, 2048])
        rstd = sb("rstd", [128, 128])
        ynT = sb("ynT", [128, 2048], BF16); mT = sb("mT", [128, 1024], BF16)
        xdtds = ynT
        H = sb("H", [128, 2048]); HTb = sb("HTb", [128, 2048], BF16)
        x1s = [sb("x1_tm%d" % i, [128, 1024]) for i in range(NX1)]; x1T = sb("x1T", [128, 1024], BF16)
        gbt = [sb("gb%d" % i, [128, 1024], BF16) for i in range(NGB)]
        ple_s = sb("ple_s", [128, 1024])
        eid_p = [sb("eid_p%d" % i, [128, 128], U32) for i in range(2)]
        gates_p = [sb("gates_p%d" % i, [128, 128]) for i in range(2)]
        lnst2 = sb("lnst2", [128, 32])
        tsum_t = sb("tsum", [128, 1024])
        dgt = [sb("dg%d" % i, [128, 128], BF16) for i in range(4)]
        identb = sb("identb", [128, 128], BF16)
        junk = sb("junk", [128, 1024], BF16)
        tk = sb("tk", [128, 2048])
        tki = sb("tki", [128, 768], U32)
        lnst = sb("lnst", [128, 32])

        v_tm = vv[:, 0:1024]; vh = v_tm
        drep, Ebc, segm, tmpL = sd[:, 0:512], sd[:, 512:1024], sd[:, 1024:1536], sd[:, 1536:2048]
        qT = xdt
        x_tm = ygT[:, 0:1024]; p_tm = ygT[:, 1024:1280]
        dtr, e1, dtt, dta, acum, alast, ecd, dsv = [sm[:, i * 32:(i + 1) * 32] for i in range(8)]
        vals = tk[:, 0:256]; wk = tk[:, 256:384]; wk2 = tk[:, 384:640]; best = tk[:, 640:768]
        paf = tk[:, 768:896]; pbf = tk[:, 896:1024]; i1f = tk[:, 1024:1280]; s1sel = tk[:, 1280:1408]; s2sel = tk[:, 1408:1536]
        eidf = tk[:, 1536:1664]; gexp = tk[:, 1664:1792]; gates = tk[:, 1792:1920]; gsm = tk[:, 1920:1936]; apre = tk[:, 1936:2048]
        idxs = tki[:, 0:256]; pos = tki[:, 256:384]; pa = tki[:, 384:512]; pb = tki[:, 512:640]; eid = tki[:, 640:768]
        wts = sb("wts", [128, 128]); gact = sb("gact", [128, 128]); apre = sb("apre", [128, 128])
        cand = ygT[:, 1024:1280]; oh = ygT[:, 1280:1536]; oh2 = ygT[:, 1536:1792]

        def bc(ap_, shape):
            return ap_.to_broadcast(shape)

        def v3(ap_, a, b):
            return ap_.rearrange("p (a b) -> p a b", a=a, b=b)

        def rawAP(t, offset_elems, dims):
            base = t[:, :]
            pstride = base.ap[0][0]
            return bass.AP(base.tensor, base.offset + offset_elems, [[pstride, 128]] + [[s, c] for s, c in dims])

        plan = []
        for ci in range(nchunks):
            for c0 in range(0, 8192, 512):
                plan.append(("win", w_in, 0, 8, c0, 512))
            plan.append(("win", w_in, 0, 8, 8192, 32))
            for c0 in range(8224, 10272, 512):
                plan.append(("win", w_in, 0, 8, c0, 512))
            for c0 in (0, 512):
                plan.append(("gm", w_gm, 0, 8, c0, 512))
            for c0 in (0, 512):
                plan.append(("sso", w_sso, 0, 8, c0, 512))
                plan.append(("sso", w_sso, 1024, 8, c0, 512))
            for c0 in (0, 512):
                plan.append(("wo", w_o, 0, 8, c0, 512))
            for c0 in range(0, 2048, 512):
                plan.append(("wq", w_q, 0, 8, c0, 512))
            for c0 in (0, 512):
                plan.append(("pg", w_pg, 0, 8, c0, 512))
            plan.append(("pp", w_pp, 0, 2, 0, 1024))
        wstate = {"issued": 0, "next": 0}

        NU = len(plan) // nchunks
        wscr = nc.dram_tensor("wscr", [NU, 128, 4096], BF16, kind="Internal").ap()

        def w_convert():
            for n in range(NU):
                tag, W, k0, nk, c0, ncol = plan[n]
                t = wr[n % NW]
                dst = t[:, 0:nk * ncol].rearrange("p (k n) -> p k n", k=nk)
                src = W[k0:k0 + nk * 128, c0:c0 + ncol].rearrange("(k p) n -> p k n", p=128)
                P.dma("pool", dst, src, writes=["wr%d" % (n % NW)])
                P.dma("sp", wscr[n, :, 0:nk * ncol], t[:, 0:nk * ncol], reads=["wr%d" % (n % NW)], writes=["wscr%d" % n])

        def w_issue(upto):
            while wstate["issued"] <= min(upto, len(plan) - 1):
                n = wstate["issued"]
                tag, W, k0, nk, c0, ncol = plan[n]
                t = wr[n % NW]
                u = n % NU
                P.dma("sp", t[:, 0:nk * ncol], wscr[u, :, 0:nk * ncol], reads=["wscr%d" % u], writes=["wr%d" % (n % NW)])
                wstate["issued"] += 1

        def w_next(tag, hold=0):
            n = wstate["next"]
            wstate["next"] += 1
            assert plan[n][0] == tag, (plan[n][0], tag)
            w_issue(n + NW - 1 - hold)
            nk, ncol = plan[n][3], plan[n][5]
            t = wr[n % NW]
            return t[:, 0:nk * ncol].rearrange("p (k n) -> p k n", k=nk), "wr%d" % (n % NW)

        def mm(out_, lhsT, rhs, start, stop, r, w):
            P.op("pe", lambda e: e.matmul(out_, lhsT, rhs, start=start, stop=stop), reads=r, writes=w)

        def tr(out_, in_, r, w):
            P.op("pe", lambda e: e.transpose(out_, in_, C("ident")), reads=list(r) + ["cps"], writes=w)

        def actf(out_, in_, func, r, w, **kw):
            P.op("act", lambda e: e.activation(out_, in_, func, **kw), reads=r, writes=w)

        def tt(eng, out_, in0, in1, op, r, w):
            P.op(eng, lambda e: e.tensor_tensor(out_, in0, in1, op), reads=r, writes=w)

        def ts(eng, out_, in0, s1, s2, op0, op1, r, w):
            if op1 is None:
                P.op(eng, lambda e: e.tensor_scalar(out_, in0, s1, None, op0), reads=r, writes=w)
            else:
                P.op(eng, lambda e: e.tensor_scalar(out_, in0, s1, s2, op0, op1), reads=r, writes=w)

        def stt(out_, in0, scalar, in1, op0, op1, r, w, accum_out=None):
            if accum_out is None:
                P.op("dve", lambda e: e.scalar_tensor_tensor(out_, in0, scalar, in1, op0, op1), reads=r, writes=w)
            else:
                P.op("dve", lambda e: e.scalar_tensor_tensor(out_, in0, scalar, in1, op0, op1, accum_out=accum_out), reads=r, writes=w)

        def cpy(eng, out_, in_, r, w):
            if eng == "act":
                P.op("act", lambda e: e.copy(out_, in_), reads=r, writes=w)
            else:
                P.op(eng, lambda e: e.tensor_copy(out_, in_), reads=r, writes=w)

        def layer_norm(src, dst, tmp, gname, bname, rsrc, rdst, rtmp, lnst=lnst, rl="lnst"):
            for h_ in range(2):
                P.op("dve", lambda e, h_=h_: e.bn_stats(lnst[:, h_ * 6:(h_ + 1) * 6], src[:, h_ * 512:(h_ + 1) * 512]),
                     reads=[rsrc], writes=[rl])
            P.op("dve", lambda e: e.bn_aggr(lnst[:, 12:14], lnst[:, 0:12]), reads=[rl], writes=[rl])
            ts("dve", lnst[:, 14:15], lnst[:, 13:14], LN_EPS, None, ALU.add, None, [rl], [rl])
            P.op("act", lambda e: e.sqrt(lnst[:, 15:16], lnst[:, 14:15]), reads=[rl], writes=[rl])
            P.op("dve", lambda e: e.reciprocal(lnst[:, 16:17], lnst[:, 15:16]), reads=[rl], writes=[rl])
            ts("dve", tmp, src, lnst[:, 12:13], lnst[:, 16:17], ALU.subtract, ALU.mult, [rsrc, rl], [rtmp])
            tt("dve", tmp, tmp, C(gname), ALU.mult, [rtmp, "cps"], [rtmp])
            tt("dve", dst, tmp, C(bname), ALU.add, [rtmp, "cps"], [rdst])

        taps_done = []

        def tap(name, src_ap, res):
            if name in tapo and name not in tapped:
                tapped.add(name)
                taps_done.append(P.dma("pool", tapo[name], src_ap, reads=[res]))

        tapped = set()

        def stage(name):
            pass

        P.dma("sp", cps[:, :], cp, writes=["cps"])
        P.dma("pool", kT[:, :], kT_in, writes=["kT"])
        P.dma("sp", vv[:, 0:1024], wsT_in, writes=["vv"])
        tt("dve", v3(wsT[:, :], 8, 128), v3(vv[:, 0:1024], 8, 128),
           bc(C("tri").rearrange("p (o t) -> p o t", o=1), [128, 8, 128]), ALU.mult, ["vv", "cps"], ["wsT"])
        actf(a_bc[:, :], C("alog"), AF.Exp, ["cps"], ["a_bc"])
        ts("dve", a_bc[:, :], a_bc[:, :], -1.0, None, ALU.mult, None, ["a_bc"], ["a_bc"])

        cpy("dve", identb[:, :], C("ident"), ["cps"], ["identb"])
        w_convert()
        uscr = nc.dram_tensor("uscr", [16384, 1024], BF16, kind="Internal").ap()
        vscr = nc.dram_tensor("vscr", [16384, 1024], BF16, kind="Internal").ap()
        stg = [(xdt, "xdt"), (HTb, "HTb"), (ynT, "ynT"), (gT, "gT"), (szT, "szT"), (BCb, "BCb")]
        tab_stores = []
        n_ = 0
        for src_t, dst_t in ((peer_u, uscr), (peer_v, vscr)):
            s3 = src_t.rearrange("(p j) d -> p j d", j=128)
            d3 = dst_t.rearrange("(p j) d -> p j d", j=128)
            for sl in range(64):
                tl, tr_ = stg[n_ % len(stg)]
                n_ += 1
                P.dma("pool", tl[:, :].rearrange("p (j d) -> p j d", j=2), s3[:, 2 * sl:2 * sl + 2, :], writes=[tr_])
                tab_stores.append(P.dma("sp", d3[:, 2 * sl:2 * sl + 2, :], tl[:, :].rearrange("p (j d) -> p j d", j=2), reads=[tr_]))
        P.barrier_wait("pool", tab_stores)
        out_dmas = []

        a_split = [0]

        def phaseA(ci):
          if True:
              x1_tm = x1s[ci % NX1]; x1r = "x1_tm%d" % (ci % NX1)
              c_in_seq = ci % nch
              t0 = ci * 128
              if c_in_seq == 0:
                  P.op("pool", lambda e: e.memset(H[:, :], 0.0), writes=["H"])
                  P.op("pool", lambda e: e.memset(HTb[:, :], 0.0), writes=["HTb"])
                  P.op("pool", lambda e: e.memset(halo[:, :], 0.0), writes=["halo"])
              P.dma("sp", x_tm[:, :], x[t0:t0 + 128, :], writes=["ygT"])
              P.dma("sp", p_tm[:, :], p[t0:t0 + 128, :], writes=["ygT"])
              for h_ in range(2):
                  b = nb()
                  for j in range(4):
                      tr(ps[b][:, blk(j)], x_tm[:, blk(h_ * 4 + j)], ["ygT"], ["ps%d" % b])
                  cpy("act" if h_ else "dve", xT[:, h_ * 512:(h_ + 1) * 512], ps[b][:, :], ["ps%d" % b], ["xT"])
              b = nb()
              for j in range(2):
                  tr(ps[b][:, blk(j)], p_tm[:, blk(j)], ["ygT"], ["ps%d" % b])
              cpy("dve", pT[:, :], ps[b][:, 0:256], ["ps%d" % b], ["pT"])

              for un in range(2):
                  wv, wres = w_next("win")
                  b = nb()
                  for j in range(4):
                      for kc in range(8):
                          mm(ps[b][:, blk(j)], wv[:, kc, blk(j)], xT[:, blk(kc)], kc == 0, kc == 7, [wres, "xT"], ["ps%d" % b])
                  actf(uT[:, un * 512:(un + 1) * 512], ps[b][:, :], AF.Gelu_apprx_tanh, ["ps%d" % b], ["uT"])
              for un in range(2):
                  wv, wres = w_next("win")
                  b = nb()
                  for kc in range(8):
                      mm(ps[b][:, :], xT[:, blk(kc)], wv[:, kc, :], kc == 0, kc == 7, [wres, "xT"], ["ps%d" % b])
                  actf(vv[:, un * 512:(un + 1) * 512], ps[b][:, :], AF.Gelu_apprx_tanh, ["ps%d" % b], ["vv"])
              for un in range(4):
                  wv, wres = w_next("win")
                  b = nb()
                  for j in range(4):
                      for kc in range(8):
                          mm(ps[b][:, blk(j)], wv[:, kc, blk(j)], xT[:, blk(kc)], kc == 0, kc == 7, [wres, "xT"], ["ps%d" % b])
                  actf(szT[:, un * 512:(un + 1) * 512], ps[b][:, :], AF.Silu, ["ps%d" % b], ["szT"])
              for un in range(8):
                  wv, wres = w_next("win")
                  b = nb()
                  for j in range(4):
                      for kc in range(8):
                          mm(ps[b][:, blk(j)], wv[:, kc, blk(j)], xT[:, blk(kc)], kc == 0, kc == 7, [wres, "xT"], ["ps%d" % b])
                  rt = rawt[un % 2]
                  rres = "rawt%d" % (un % 2)
                  r3 = v3(rt[:, :], 4, 131)
                  cpy("act", r3[:, :, 0:3], v3(halo[:, un * 12:(un + 1) * 12], 4, 3), ["halo"], [rres])
                  cpy("act", r3[:, :, 3:131], v3(ps[b][:, :], 4, 128), ["ps%d" % b], [rres])
                  cpy("act", v3(halo[:, un * 12:(un + 1) * 12], 4, 3), r3[:, :, 128:131], [rres], ["halo"])
                  cc = cconv[un % 2]
                  ccr = ["cconv%d_%d" % (un % 2, j) for j in range(4)]
                  for j in range(4):
                      jj = un * 4 + j
                      cw = lambda k, jj=jj: C("convw", jj * 4 + k, jj * 4 + k + 1)
                      actf(cc[:, blk(j)], rt[:, j * 131 + 3:j * 131 + 131], AF.Identity, [rres, "cps"], [ccr[j]],
                           bias=C("convb", jj, jj + 1), scale=cw(3))
                  for k in (2, 1, 0):
                      for j in range(4):
                          jj = un * 4 + j
                          cw = lambda k, jj=jj: C("convw", jj * 4 + k, jj * 4 + k + 1)
                          stt(cc[:, blk(j)], rt[:, j * 131 + k:j * 131 + k + 128], cw(k), cc[:, blk(j)], ALU.mult, ALU.add,
                              [rres, "cps", ccr[j]], [ccr[j]])
                  if un < 6:
                      qd = xq[un // 2]
                      qres = "xq%d" % (un // 2)
                      actf(qd[:, (un % 2) * 512:(un % 2 + 1) * 512], cc[:, :], AF.Silu, ccr, [qres])
                      if un >= 4:
                          cpy("act", BCb[:, (un - 4) * 512:(un - 3) * 512], qd[:, (un % 2) * 512:(un % 2 + 1) * 512], [qres], ["BCb"])
                  else:
                      actf(BCb[:, (un - 4) * 512:(un - 3) * 512], cc[:, :], AF.Silu, ccr, ["BCb"])
              wv, wres = w_next("win")
              b = nb()
              for kc in range(8):
                  mm(ps[b][:, 0:32], xT[:, blk(kc)], wv[:, kc, :], kc == 0, kc == 7, [wres, "xT"], ["ps%d" % b])
              tt("dve", dtr, ps[b][:, 0:32], C("dtb"), ALU.add, ["ps%d" % b, "cps"], ["sm_dt"])
              actf(e1, dtr, AF.Exp, ["sm_dt"], ["sm_dt"])
              actf(dtt, e1, AF.Ln, ["sm_dt"], ["sm_dt"], bias=C("ones", 0, 1), scale=1.0)
              tt("dve", dta, dtt, a_bc[:, :], ALU.mult, ["sm_dt", "a_bc"], ["sm_dt"])
              for un in range(4):
                  wv, wres = w_next("win")
                  b = nb()
                  for j in range(4):
                      for kc in range(8):
                          mm(ps[b][:, blk(j)], wv[:, kc, blk(j)], xT[:, blk(kc)], kc == 0, kc == 7, [wres, "xT"], ["ps%d" % b])
                  for j in range(4):
                      jj = un * 4 + j
                      actf(gT[:, blk(jj)], ps[b][:, blk(j)], AF.Sigmoid, ["ps%d" % b, "cps"], ["gT"],
                           bias=C("bgate", jj, jj + 1), scale=1.0)

              tap("uT", uT[:, :], "uT"); tap("v", vv[:, 0:1024], "vv"); tap("gT", gT[:, :], "gT"); tap("szT", szT[:, :], "szT")
              tap("xq0", xq[0][:, :], "xq0"); tap("dt", dtt, "sm_dt")
              stage("win")
              layer_norm(v_tm, vnb[:, :], vh, "gmg", "gmb", "vv", "vnb", "vv")
              for h_ in range(2):
                  b = nb()
                  for j in range(4):
                      g = h_ * 4 + j
                      mm(ps[b][:, blk(j)], vnb[:, blk(g)], wsT[:, blk(g)], True, True, ["vnb", "wsT"], ["ps%d" % b])
                  tt("dve", cacc[:, :], ps[b][:, :], C("bs", h_ * 512, (h_ + 1) * 512), ALU.add, ["ps%d" % b, "cps"], ["cacc"])
                  tt("dve", gaT[:, h_ * 512:(h_ + 1) * 512], cacc[:, :], uT[:, h_ * 512:(h_ + 1) * 512], ALU.mult, ["cacc", "uT"], ["gaT"])
              for h_ in range(2):
                  wv, wres = w_next("gm")
                  b = nb()
                  for j in range(4):
                      for kc in range(8):
                          mm(ps[b][:, blk(j)], wv[:, kc, blk(j)], gaT[:, blk(kc)], kc == 0, kc == 7, [wres, "gaT"], ["ps%d" % b])
                  tt("dve", mA[:, h_ * 512:(h_ + 1) * 512], ps[b][:, :], gT[:, h_ * 512:(h_ + 1) * 512], ALU.mult, ["ps%d" % b, "gT"], ["mA"])

              tap("mA", mA[:, :], "mA")
              stage("gmlp")
              b = nb()
              mm(ps[b][:, 0:32], C("tri"), dta, True, True, ["cps", "sm_dt"], ["ps%d" % b])
              cpy("dve", acum, ps[b][:, 0:32], ["ps%d" % b], ["sm_acum"])
              for qd_i in range(2):
                  for h_ in range(2):
                      b = nb()
                      for j in range(4):
                          tr(ps[b][:, blk(j)], xq[qd_i][:, blk(h_ * 4 + j)], ["xq%d" % qd_i], ["ps%d" % b])
                      hs = (qd_i * 2 + h_) * 8
                      tt("dve", v3(xdt[:, hs * 64:(hs + 8) * 64], 8, 64), v3(ps[b][:, :], 8, 64),
                         bc(dtt[:, hs:hs + 8].rearrange("p (h o) -> p h o", o=1), [128, 8, 64]), ALU.mult, ["ps%d" % b, "sm_dt"], ["xdt"])
              for h_ in range(2):
                  b = nb()
                  for j in range(4):
                      g = h_ * 4 + j
                      mm(ps[b][:, blk(j)], BCb[:, blk(g)], BCb[:, blk(8 + g)], True, True, ["BCb"], ["ps%d" % b])
                  cpy("act", cbT[:, h_ * 512:(h_ + 1) * 512], ps[b][:, :], ["ps%d" % b], ["cbT"])
              stage("ssd_a")
              for g in range(8):
                  cpy("act", v3(drep, 4, 128), bc(dta[:, 4 * g:4 * g + 4].rearrange("p (h o) -> p h o", o=1), [128, 4, 128]), ["sm_dt"], ["sd_drep"])
                  ba = nb()
                  for hh in range(4):
                      mm(ps[ba][:, blk(hh)], drep[:, blk(hh)], C("tri"), True, True, ["sd_drep", "cps"], ["ps%d" % ba])
                  actf(Ebc, ps[ba][:, :], AF.Exp, ["ps%d" % ba], ["sd_Ebc"])
                  cpy("dve", alast[:, 4 * g:4 * g + 4], rawAP(ps[ba], 127, [(128, 4)]), ["ps%d" % ba], ["sm_alast"])
                  cpy("act", ecd[:, 4 * g:4 * g + 4], rawAP(sd, 512 + 127, [(128, 4)]), ["sd_Ebc"], ["sm_ecd"])
                  for hh in range(4):
                      h = 4 * g + hh
                      stt(segm[:, blk(hh)], ps[ba][:, blk(hh)], acum[:, h:h + 1], C("negm"), ALU.subtract, ALU.add,
                          ["ps%d" % ba, "sm_acum", "cps"], ["sd_segm"])
                  actf(tmpL, segm, AF.Exp, ["sd_segm"], ["sd_tmpL"])
                  tt("dve", v3(MT[:, :], 4, 128), v3(tmpL, 4, 128),
                     bc(cbT[:, blk(g)].rearrange("p (o t) -> p o t", o=1), [128, 4, 128]), ALU.mult, ["sd_tmpL", "cbT"], ["MT"])
                  tt("dve", v3(CsT[:, :], 4, 128), bc(BCb[:, blk(8 + g)].rearrange("p (o t) -> p o t", o=1), [128, 4, 128]),
                     v3(Ebc, 4, 128), ALU.mult, ["BCb", "sd_Ebc"], ["CsT"])
                  for q in range(2):
                      j = g * 2 + q
                      by = nb()
                      for e_ in range(2):
                          hh = 2 * q + e_
                          h = 4 * g + hh
                          o_ = ps[by][e_ * 64:(e_ + 1) * 64, 0:128]
                          mm(o_, xdt[:, h * 64:(h + 1) * 64], MT[:, blk(hh)], True, False, ["xdt", "MT"], ["ps%d" % by])
                          mm(o_, HTb[:, h * 64:(h + 1) * 64], CsT[:, blk(hh)], False, True, ["HTb", "CsT"], ["ps%d" % by])
                      xs_blk = xq[j // 8][:, blk(j % 8)]
                      yt = ytmp[j % 2]
                      stt(yt[:, :], xs_blk, C("dsk", j, j + 1), ps[by][:, 0:128], ALU.mult, ALU.add,
                          ["xq%d" % (j // 8), "cps", "ps%d" % by], ["ytmp%d" % (j % 2)])
                      tt("dve", ygT[:, blk(j)], yt[:, :], szT[:, blk(j)], ALU.mult, ["ytmp%d" % (j % 2), "szT"], ["ygT"])
                      actf(ysq[j % 2][:, :], ygT[:, blk(j)], AF.Square, ["ygT"], ["ysq%d" % (j % 2)])
                      mm(ps[7][:, 0:128], C("ones"), ysq[j % 2][:, :], j == 0, j == 15, ["cps", "ysq%d" % (j % 2)], ["ps7"])
              tap("ygT", ygT[:, :], "ygT")
              stage("ssd_g")
              ts("dve", rstd[:, :], ps[7][:, 0:128], 1.0 / 2048.0, LN_EPS, ALU.mult, ALU.add, ["ps7"], ["rstd"])
              P.op("act", lambda e: e.sqrt(rstd[:, :], rstd[:, :]), reads=["rstd"], writes=["rstd"])
              P.op("dve", lambda e: e.reciprocal(rstd[:, :], rstd[:, :]), reads=["rstd"], writes=["rstd"])
              for j in range(16):
                  stt(ynT[:, blk(j)], ygT[:, blk(j)], C("nw", j, j + 1), rstd[:, :], ALU.mult, ALU.mult, ["ygT", "cps", "rstd"], ["ynT"])
              for h_ in range(2):
                  wv0, wres0 = w_next("sso")
                  wv1, wres1 = w_next("sso", hold=1)
                  b = nb()
                  for j in range(4):
                      for kc in range(16):
                          wv, wres = (wv0, wres0) if kc < 8 else (wv1, wres1)
                          mm(ps[b][:, blk(j)], wv[:, kc % 8, blk(j)], ynT[:, blk(kc)], kc == 0, kc == 15, [wres, "ynT"], ["ps%d" % b])
                  tt("dve", cacc[:, :], ps[b][:, :], gT[:, 1024 + h_ * 512:1024 + (h_ + 1) * 512], ALU.mult, ["ps%d" % b, "gT"], ["cacc"])
                  tt("dve", mT[:, h_ * 512:(h_ + 1) * 512], cacc[:, :], mA[:, h_ * 512:(h_ + 1) * 512], ALU.add, ["cacc", "mA"], ["mT"])
              tap("mT", mT[:, :], "mT")
              stage("ssd_n")
              for h_ in range(2):
                  b = nb()
                  for j in range(4):
                      tr(ps[b][:, blk(j)], xq[2][:, blk(h_ * 4 + j)], ["xq2"], ["ps%d" % b])
                  cpy("act", B_tm[:, h_ * 512:(h_ + 1) * 512], ps[b][:, :], ["ps%d" % b], ["B_tm"])
              tt("dve", dsv, alast, acum, ALU.subtract, ["sm_alast", "sm_acum"], ["sm_dsv"])
              actf(dsv, dsv, AF.Exp, ["sm_dsv"], ["sm_dsv"])
              tt("dve", v3(xdtds[:, :], 32, 64), v3(xdt[:, :], 32, 64),
                 bc(dsv.rearrange("p (h o) -> p h o", o=1), [128, 32, 64]), ALU.mult, ["xdt", "sm_dsv"], ["ynT"])
              for gp in range(4):
                  b = nb()
                  for e_ in range(2):
                      g = gp * 2 + e_
                      mm(ps[b][:, e_ * 256:(e_ + 1) * 256], B_tm[:, blk(g)], xdtds[:, g * 256:(g + 1) * 256], True, True,
                         ["B_tm", "ynT"], ["ps%d" % b])
                  hs = gp * 8
                  tt("dve", v3(H[:, gp * 512:(gp + 1) * 512], 8, 64), v3(H[:, gp * 512:(gp + 1) * 512], 8, 64),
                     bc(ecd[:, hs:hs + 8].rearrange("p (h o) -> p h o", o=1), [128, 8, 64]), ALU.mult, ["H", "sm_ecd"], ["H"])
                  tt("dve", H[:, gp * 512:(gp + 1) * 512], H[:, gp * 512:(gp + 1) * 512], ps[b][:, :], ALU.add, ["H", "ps%d" % b], ["H"])
                  cpy("act", HTb[:, gp * 512:(gp + 1) * 512], H[:, gp * 512:(gp + 1) * 512], ["H"], ["HTb"])

              tap("ygT", ygT[:, :], "ygT"); tap("mT", mT[:, :], "mT"); tap("H", H[:, :], "H")
              stage("ssd")
              x1pre = ygT[:, 0:1024]
              P.dma("sp", x1pre, x[t0:t0 + 128, :], writes=["ygT"])
              for h_ in range(2):
                  wv, wres = w_next("wo")
                  b = nb()
                  for kc in range(8):
                      mm(ps[b][:, :], mT[:, blk(kc)], wv[:, kc, :], kc == 0, kc == 7, [wres, "mT"], ["ps%d" % b])
                  stt(x1pre[:, h_ * 512:(h_ + 1) * 512], x1pre[:, h_ * 512:(h_ + 1) * 512], ALPHA, ps[b][:, :], ALU.mult, ALU.add,
                      ["ygT", "ps%d" % b], ["ygT"])
              layer_norm(x1pre, x1_tm[:, :], x1pre, "ln1g", "ln1b", "ygT", x1r, "ygT")
              tap("x1", x1_tm[:, :], x1r)
              for h_ in range(2):
                  b = nb()
                  for j in range(4):
                      tr(ps[b][:, blk(j)], x1_tm[:, blk(h_ * 4 + j)], [x1r], ["ps%d" % b])
                  cpy("act" if h_ else "dve", x1T[:, h_ * 512:(h_ + 1) * 512], ps[b][:, :], ["ps%d" % b], ["x1T"])

              stage("ln1")
              for un in range(4):
                  wv, wres = w_next("wq")
                  b = nb()
                  for j in range(4):
                      for kc in range(8):
                          mm(ps[b][:, blk(j)], wv[:, kc, blk(j)], x1T[:, blk(kc)], kc == 0, kc == 7, [wres, "x1T"], ["ps%d" % b])
                  cpy("act", qT[:, un * 512:(un + 1) * 512], ps[b][:, :], ["ps%d" % b], ["xdt"])
              for bi in range(4):
                  b = nb()
                  for j in range(4):
                      i = bi * 4 + j
                      mm(ps[b][:, blk(j)], qT[:, blk(i)], kT[:, blk(i)], True, True, ["xdt", "kT"], ["ps%d" % b])
                  for j in range(4):
                      i = bi * 4 + j
                      s_i = ps[b][:, blk(j)]
                      r_ = ["ps%d" % b]
                      P.op("dve", lambda e, s_i=s_i, i=i: e.max(vals[:, i * 16:i * 16 + 8], s_i), reads=r_, writes=["tk"])
                      P.op("dve", lambda e, s_i=s_i, i=i: e.match_replace(wk, vals[:, i * 16:i * 16 + 8], s_i, -1e30), reads=r_ + ["tk"], writes=["tk"])
                      P.op("dve", lambda e, i=i: e.max(vals[:, i * 16 + 8:i * 16 + 16], wk), reads=["tk"], writes=["tk"])
                      P.op("dve", lambda e, s_i=s_i, i=i: e.max_index(idxs[:, i * 16:i * 16 + 8], vals[:, i * 16:i * 16 + 8], s_i), reads=r_ + ["tk"], writes=["tki"])
                      P.op("dve", lambda e, s_i=s_i, i=i: e.max_index(idxs[:, i * 16 + 8:i * 16 + 16], vals[:, i * 16 + 8:i * 16 + 16], s_i), reads=r_ + ["tk"], writes=["tki"])
              cpy("dve", i1f, idxs, ["tki"], ["tk"])
              for h in range(8):
                  tt("dve", v3(cand, 16, 16),
                     bc(vals[:, (2 * h) * 16:(2 * h) * 16 + 16].rearrange("p (a o) -> p a o", o=1), [128, 16, 16]),
                     bc(vals[:, (2 * h + 1) * 16:(2 * h + 1) * 16 + 16].rearrange("p (o b) -> p o b", o=1), [128, 16, 16]),
                     ALU.add, ["tk"], ["ygT"])
                  bh = best[:, h * 16:(h + 1) * 16]
                  ph = pos[:, h * 16:(h + 1) * 16]
                  P.op("dve", lambda e, bh=bh: e.max(bh[:, 0:8], cand), reads=["ygT"], writes=["tk"])
                  P.op("dve", lambda e, bh=bh: e.match_replace(wk2, bh[:, 0:8], cand, -1e30), reads=["ygT", "tk"], writes=["tk"])
                  P.op("dve", lambda e, bh=bh: e.max(bh[:, 8:16], wk2), reads=["tk"], writes=["tk"])
                  P.op("dve", lambda e, bh=bh, ph=ph: e.max_index(ph[:, 0:8], bh[:, 0:8], cand), reads=["ygT", "tk"], writes=["tki"])
                  P.op("dve", lambda e, bh=bh, ph=ph: e.max_index(ph[:, 8:16], bh[:, 8:16], cand), reads=["ygT", "tk"], writes=["tki"])
              P.op("dve", lambda e: e.tensor_single_scalar(pa, pos, 4, ALU.logical_shift_right), reads=["tki"], writes=["tki"])
              P.op("dve", lambda e: e.tensor_single_scalar(pb, pos, 15, ALU.bitwise_and), reads=["tki"], writes=["tki"])
              cpy("dve", paf, pa, ["tki"], ["tk"])
              cpy("dve", pbf, pb, ["tki"], ["tk"])
              for h in range(8):
                  for which, pf, sel in ((0, paf, s1sel), (1, pbf, s2sel)):
                      tt("dve", v3(oh, 16, 16),
                         bc(C("iota16").rearrange("p (o a) -> p o a", o=1), [128, 16, 16]),
                         bc(pf[:, h * 16:(h + 1) * 16].rearrange("p (r o) -> p r o", o=1), [128, 16, 16]),
                         ALU.is_equal, ["cps", "tk"], ["ygT"])
                      ioff = (2 * h + which) * 16
                      tt("dve", v3(oh2, 16, 16), v3(oh, 16, 16),
                         bc(i1f[:, ioff:ioff + 16].rearrange("p (o a) -> p o a", o=1), [128, 16, 16]),
                         ALU.mult, ["ygT", "tk"], ["ygT"])
                      P.op("dve", lambda e, sel=sel, h=h: e.tensor_reduce(sel[:, h * 16:(h + 1) * 16], v3(oh2, 16, 16), AX.X, ALU.add),
                           reads=["ygT"], writes=["tk"])
              stt(eidf, s1sel, 128.0, s2sel, ALU.mult, ALU.add, ["tk"], ["tk"])
              cpy("dve", eid, eidf, ["tk"], ["tki"])
              tt("dve", gexp.rearrange("p (h r) -> p h r", h=8), best.rearrange("p (h r) -> p h r", h=8),
                 bc(rawAP(tk, 640, [(16, 8), (0, 1)]), [128, 8, 16]) if False else bc(best.rearrange("p (h r) -> p h r", h=8)[:, :, 0:1], [128, 8, 16]),
                 ALU.subtract, ["tk"], ["tk"])
              actf(gexp, gexp, AF.Exp, ["tk"], ["tk"])
              P.op("dve", lambda e: e.tensor_reduce(gsm[:, 0:8], gexp.rearrange("p (h r) -> p h r", h=8), AX.X, ALU.add), reads=["tk"], writes=["tk"])
              P.op("dve", lambda e: e.reciprocal(gsm[:, 8:16], gsm[:, 0:8]), reads=["tk"], writes=["tk"])
              tt("dve", gates.rearrange("p (h r) -> p h r", h=8), gexp.rearrange("p (h r) -> p h r", h=8),
                 bc(gsm[:, 8:16].rearrange("p (h o) -> p h o", o=1), [128, 8, 16]), ALU.mult, ["tk"], ["tk"])
              tap("eid", eidf, "tk")
              tap("gates", gates, "tk")
              par = ci % 2
              cpy("dve", eid_p[par][:, :], eid, ["tki"], ["eid_p%d" % par])
              cpy("dve", gates_p[par][:, :], gates, ["tk"], ["gates_p%d" % par])
              a_split[0] = len(P.cur)
              pg = ygT[:, 0:1024]
              for h_ in range(2):
                  wv, wres = w_next("pg")
                  b = nb()
                  for kc in range(8):
                      mm(ps[b][:, :], x1T[:, blk(kc)], wv[:, kc, :], kc == 0, kc == 7, [wres, "x1T"], ["ps%d" % b])
                  tt("dve", pg[:, h_ * 512:(h_ + 1) * 512], ps[b][:, :], C("pleb", h_ * 512, (h_ + 1) * 512), ALU.add, ["ps%d" % b, "cps"], ["ygT"])
              actf(pg, pg, AF.Sigmoid, ["ygT"], ["ygT"])
              wv, wres = w_next("pp")
              for h_ in range(2):
                  b = nb()
                  for kc in range(2):
                      mm(ps[b][:, :], pT[:, blk(kc)], wv[:, kc, h_ * 512:(h_ + 1) * 512], kc == 0, kc == 1, [wres, "pT"], ["ps%d" % b])
                  tt("dve", ple_s[:, h_ * 512:(h_ + 1) * 512], pg[:, h_ * 512:(h_ + 1) * 512], ps[b][:, :], ALU.mult, ["ygT", "ps%d" % b], ["ple_s"])

        def phaseB(ci):
          if True:
              x1_tm = x1s[ci % NX1]; x1r = "x1_tm%d" % (ci % NX1)
              par = ci % 2
              t0 = ci * 128
              eidc = eid_p[par]; er = "eid_p%d" % par
              gatc = gates_p[par]; gr = "gates_p%d" % par
              NG = NGB
              for k in range(128):
                  g_ = gbt[k % NG][:, :]
                  gres = "gb%d" % (k % NG)
                  P.op("pool", lambda e, g_=g_, k=k: e.indirect_dma_start(
                      out=g_, out_offset=None, in_=uscr,
                      in_offset=bass.IndirectOffsetOnAxis(ap=eidc[:, k:k + 1], axis=0)),
                      reads=[er], writes=[gres], dma=True)
                  stt(junk[:, :], g_, 1.0, x1_tm[:, :], ALU.mult, ALU.mult, [gres, x1r], ["junk", "apre"], accum_out=apre[:, k:k + 1])
              actf(gact[:, :], apre[:, :], AF.Gelu_apprx_tanh, ["apre"], ["gact"])
              tt("dve", wts[:, :], gact[:, :], gatc[:, :], ALU.mult, ["gact", gr], ["wts"])
              for k in range(128):
                  g_ = gbt[k % NG][:, :]
                  gres = "gb%d" % (k % NG)
                  P.op("pool", lambda e, g_=g_, k=k: e.indirect_dma_start(
                      out=g_, out_offset=None, in_=vscr,
                      in_offset=bass.IndirectOffsetOnAxis(ap=eidc[:, k:k + 1], axis=0)),
                      reads=[er], writes=[gres], dma=True)
                  dg = dgt[k % 4]
                  dres = "dg%d" % (k % 4)
                  actf(dg[:, :], identb[:, :], AF.Copy, ["identb", "wts"], [dres], scale=wts[:, k:k + 1])
                  mm(ps[5][:, :], dg[:, :], g_[:, 0:512], k == 0, k == 127, [dres, gres], ["ps5"])
                  mm(ps[6][:, :], dg[:, :], g_[:, 512:1024], k == 0, k == 127, [dres, gres], ["ps6"])
              tsum = tsum_t[:, :]
              for h_ in range(2):
                  stt(tsum[:, h_ * 512:(h_ + 1) * 512], x1_tm[:, h_ * 512:(h_ + 1) * 512], ALPHA, ps[5 + h_][:, :], ALU.mult, ALU.add,
                      [x1r, "ps%d" % (5 + h_)], ["tsum"])
              tt("dve", tsum, tsum, ple_s[:, :], ALU.add, ["tsum", "ple_s"], ["tsum"])
              layer_norm(tsum, tsum, tsum, "ln2g", "ln2b", "tsum", "tsum", "tsum", lnst=lnst2, rl="lnst2")
              out_dmas.append(P.dma("sp", out[t0:t0 + 128, :], tsum, reads=["tsum"]))

        def merge(bl, al):
            outl = []
            na, nbl = len(al), len(bl)
            ia = 0
            for ib in range(nbl):
                outl.append(bl[ib])
                tgt = ((ib + 1) * na) // nbl
                outl.extend(al[ia:tgt])
                ia = tgt
            outl.extend(al[ia:])
            return outl

        P.begin(); phaseA(0); P.extend(P.end())
        for ci in range(nchunks):
            P.begin(); phaseB(ci); bl = P.end()
            al, ae = [], []
            if ci + 1 < nchunks:
                P.begin(); phaseA(ci + 1); al = P.end()
                al, ae = al[:a_split[0]], al[a_split[0]:]
            P.extend(merge(bl, al))
            P.extend(ae)

        P.barrier_wait("sp", out_dmas + taps_done)
        P.emit()
    return nc


_NC_CACHE = {}


def kernel(**I):
    I = {k: np.asarray(v) for k, v in I.items()}
    B, S, D = I["x"].shape
    ncores = 8
    nseq = B // ncores
    nch = S // 128
    key = (nseq, nch)
    if key not in _NC_CACHE:
        _NC_CACHE[key] = build(nseq, nch)
    nc = _NC_CACHE[key]
    cp = pack_cp(I)
    wsT = np.ascontiguousarray(I["gm_w_s"][0].transpose(2, 0, 1).reshape(128, 1024))
    k1, k2 = I["peer_k1"][0], I["peer_k2"][0]
    kk = np.stack([k1, k2], axis=1)
    kT = np.ascontiguousarray(kk.transpose(3, 0, 1, 2).reshape(128, 2048))
    shared = {
        "w_in": np.ascontiguousarray(I["w_in"][0]), "gm_w_out": np.ascontiguousarray(I["gm_w_out"][0]),
        "ssd_w_out": np.ascontiguousarray(I["ssd_w_out"][0]), "w_o": np.ascontiguousarray(I["w_o"][0]),
        "w_q": np.ascontiguousarray(I["peer_w_q"][0]), "w_pg": np.ascontiguousarray(I["ple_w_gate"][0]),
        "w_pp": np.ascontiguousarray(I["ple_w_proj"][0]), "peer_u": np.ascontiguousarray(I["peer_u"][0]),
        "peer_v": np.ascontiguousarray(I["peer_v"][0]), "cp": cp, "wsT": wsT, "kT": kT,
    }
    in_maps = []
    for c in range(ncores):
        m = dict(shared)
        m["x"] = np.ascontiguousarray(I["x"][c * nseq:(c + 1) * nseq].reshape(nseq * S, D))
        m["p"] = np.ascontiguousarray(I["p"][0, c * nseq:(c + 1) * nseq].reshape(nseq * S, 256))
        in_maps.append(m)
    res = run_bass_kernel_spmd(nc, in_maps, core_ids=list(range(ncores)))
    outs = [np.asarray(r["out"]).reshape(nseq, S, D) for r in res.results]
    return np.concatenate(outs, axis=0).astype(np.float32)
```
